# Optimizing a Trainium2 kernel written in Bass

```python
import jax, jax.numpy as jnp
from jax import lax
import numpy as np

D_MODEL = 1024
BATCH = 4
SEQ = 4096
DEPTH = 4
DEC_BATCH = 2
DEC_SEQ = 16384
PAST_LEN = 128

N_BRANCHES = 4
BRANCH_WIDTH = D_MODEL // 2
HEAD_DIM = 128
N_HEADS_BRANCH = BRANCH_WIDTH // HEAD_DIM
DILATION_GROUPS = ((128, 1), (512, 4), (2048, 16))
N_DIL = len(DILATION_GROUPS)
A_HEADS_TOTAL = N_DIL * N_HEADS_BRANCH
A_QKV_WIDTH = A_HEADS_TOTAL * HEAD_DIM
REL_BUCKETS = 32
REL_MAX_DIST = 1024
FNET_GROUPS = 4
FNET_GROUP_WIDTH = BRANCH_WIDTH // FNET_GROUPS
MLSTM_CHUNK = 64
MLSTM_CONV = 5
MLSTM_GATES = 2 * 2 * N_HEADS_BRANCH
N_MEM = 256
IN_WIDTH = (3 * A_QKV_WIDTH + BRANCH_WIDTH + 4 * BRANCH_WIDTH + MLSTM_GATES + BRANCH_WIDTH
            + N_BRANCHES * BRANCH_WIDTH + N_BRANCHES * D_MODEL)
EPS = 1e-6
NEG = -1e30

kernel_name = "hybrid_dilated_fnet_mlstm_encoder"


def rms_norm(x, g):
    xf = x.astype(jnp.float32)
    y = xf * lax.rsqrt(jnp.mean(xf * xf, axis=-1, keepdims=True) + EPS)
    return (y * g.astype(jnp.float32)).astype(x.dtype)


def t5_bucket(rel):
    nb = REL_BUCKETS // 2
    max_exact = nb // 2
    ret = (rel > 0).astype(np.int32) * nb
    n = np.abs(rel)
    large = max_exact + (np.log(np.maximum(n, 1) / max_exact) / np.log(REL_MAX_DIST / max_exact)
                         * (nb - max_exact)).astype(np.int32)
    large = np.minimum(large, nb - 1)
    return (ret + np.where(n < max_exact, n, large)).astype(np.int32)


def band_bias(rel_bias_heads, dilation, half):
    rel = np.arange(3 * half)[None, :] - half - np.arange(half)[:, None]
    idx = t5_bucket(dilation * rel)
    return jnp.transpose(rel_bias_heads[idx], (2, 0, 1))


def dilated_window_attention(q, k, v, dilation, half, bias):
    B, L, H, C = q.shape
    M = L // dilation
    nb = -(-M // half)
    Mp = nb * half

    def to_sub(t):
        return t.reshape(B, M, dilation, H, C).transpose(0, 2, 3, 1, 4)

    def key_windows(t):
        tp = jnp.pad(t, ((0, 0), (0, 0), (0, 0), (half, Mp - M + half), (0, 0)))
        tp = tp.reshape(B, dilation, H, nb + 2, half, C)
        return jnp.concatenate([tp[:, :, :, 0:nb], tp[:, :, :, 1:nb + 1], tp[:, :, :, 2:nb + 2]], axis=4)

    qb = jnp.pad(to_sub(q), ((0, 0), (0, 0), (0, 0), (0, Mp - M), (0, 0))).reshape(B, dilation, H, nb, half, C)
    kw = key_windows(to_sub(k))
    vw = key_windows(to_sub(v))
    s = jnp.einsum('brhnqc,brhnkc->brhnqk', qb, kw).astype(jnp.float32) * (C ** -0.5)
    s = s + bias[None, None, :, None].astype(jnp.float32)
    rel = np.arange(3 * half)[None, :] - half - np.arange(half)[:, None]
    band = np.abs(rel) <= half
    kidx = np.arange(nb)[:, None] * half - half + np.arange(3 * half)[None, :]
    valid = (kidx >= 0) & (kidx < M)
    mask = jnp.asarray(band[None] & valid[:, None, :])
    s = jnp.where(mask, s, NEG)
    lse = jax.nn.logsumexp(s, axis=-1)
    p = jnp.exp(s - lse[..., None])
    o = jnp.einsum('brhnqk,brhnkc->brhnqc', p.astype(vw.dtype), vw)
    o = o.reshape(B, dilation, H, Mp, C)[:, :, :, :M].transpose(0, 3, 1, 2, 4).reshape(B, L, H, C)
    lse = lse.reshape(B, dilation, H, Mp)[..., :M].transpose(0, 3, 1, 2).reshape(B, L, H)
    return o, lse


def centred_conv(x, w):
    K = w.shape[0]
    pad = K // 2
    L = x.shape[1]
    xp = jnp.pad(x, ((0, 0), (pad, pad), (0, 0)))
    return sum(xp[:, j:j + L] * w[j] for j in range(K))


def mlstm_chunked(q, k, v, i_pre, f_pre):
    B, H, L, dh = q.shape
    T = MLSTM_CHUNK
    N = L // T
    q = q.reshape(B, H, N, T, dh) * (dh ** -0.5)
    k = k.reshape(B, H, N, T, dh)
    v = v.reshape(B, H, N, T, dh)
    ig = i_pre.reshape(B, H, N, T)
    b = jnp.cumsum(jax.nn.log_sigmoid(f_pre).reshape(B, H, N, T), axis=-1)
    g = b[..., -1]
    causal = jnp.tril(jnp.ones((T, T), dtype=bool))
    logd = jnp.where(causal, b[..., :, None] - b[..., None, :] + ig[..., None, :], NEG)
    a = g[..., None] - b + ig
    ma = jnp.max(a, axis=-1)
    wa = jnp.exp(a - ma[..., None])
    c_loc = jnp.einsum('bhnt,bhntd,bhnte->bhnde', wa, k, v)
    n_loc = jnp.einsum('bhnt,bhntd->bhnd', wa, k)

    def step(carry, inp):
        c, n, m = carry
        g_c, ma_c, c_l, n_l = inp
        m_new = jnp.maximum(g_c + m, ma_c)
        s_old = jnp.exp(g_c + m - m_new)
        s_new = jnp.exp(ma_c - m_new)
        c_new = s_old[..., None, None] * c + s_new[..., None, None] * c_l
        n_new = s_old[..., None] * n + s_new[..., None] * n_l
        return (c_new, n_new, m_new), (c, n, m)

    init = (jnp.zeros((B, H, dh, dh), jnp.float32), jnp.zeros((B, H, dh), jnp.float32),
            jnp.zeros((B, H), jnp.float32))
    xs = (jnp.moveaxis(g, 2, 0), jnp.moveaxis(ma, 2, 0), jnp.moveaxis(c_loc, 2, 0), jnp.moveaxis(n_loc, 2, 0))
    _, (c_prev, n_prev, m_prev) = lax.scan(step, init, xs)
    c_prev = jnp.moveaxis(c_prev, 0, 2)
    n_prev = jnp.moveaxis(n_prev, 0, 2)
    m_prev = jnp.moveaxis(m_prev, 0, 2)
    inter = b + m_prev[..., None]
    m_t = jnp.maximum(inter, jnp.max(logd, axis=-1))
    w_intra = jnp.exp(logd - m_t[..., None])
    w_inter = jnp.exp(inter - m_t)
    sqk = w_intra * jnp.einsum('bhntd,bhnsd->bhnts', q, k)
    num = (w_inter[..., None] * jnp.einsum('bhntd,bhnde->bhnte', q, c_prev)
           + jnp.einsum('bhnts,bhnse->bhnte', sqk, v))
    den = w_inter * jnp.einsum('bhntd,bhnd->bhnt', q, n_prev) + jnp.sum(sqk, axis=-1)
    h = num / jnp.maximum(jnp.abs(den), jnp.exp(-m_t))[..., None]
    return h.reshape(B, H, L, dh)


def encoder_layer(x, mem, rel_bias, norm_pre, w_in, conv_qk, gate_bias, head_gain, mem_norm,
                  w_mem_kv, w_branch, w_out, norm_post):
    B, L, _ = x.shape
    H, dh, W = N_HEADS_BRANCH, HEAD_DIM, BRANCH_WIDTH
    h = rms_norm(x, norm_pre)
    z = h @ w_in
    widths = [A_QKV_WIDTH] * 3 + [W] + [W] * 4 + [MLSTM_GATES, W, N_BRANCHES * W, N_BRANCHES * D_MODEL]
    splits = np.cumsum(widths)[:-1].tolist()
    qa, ka, va, xb, qc, kc, vc, oc, gc, qd, gp, mg = jnp.split(z, splits, axis=-1)

    qa = qa.reshape(B, L, N_DIL, H, dh)
    ka = ka.reshape(B, L, N_DIL, H, dh)
    va = va.reshape(B, L, N_DIL, H, dh)
    outs, lses = [], []
    for gi, (window, dilation) in enumerate(DILATION_GROUPS):
        half = window // (2 * dilation)
        bias = band_bias(rel_bias[:, gi * H:(gi + 1) * H], dilation, half)
        o, l = dilated_window_attention(qa[:, :, gi], ka[:, :, gi], va[:, :, gi], dilation, half, bias)
        outs.append(o.astype(jnp.float32))
        lses.append(l)
    wgt = jax.nn.softmax(jnp.stack(lses, axis=0), axis=0)
    ya = jnp.einsum('gblh,gblhc->blhc', wgt, jnp.stack(outs, axis=0)).reshape(B, L, W)

    xb4 = xb.reshape(B, L, FNET_GROUPS, FNET_GROUP_WIDTH).astype(jnp.float32)
    yb = jnp.fft.fft2(xb4, axes=(1, 3), norm="ortho").real.reshape(B, L, W)

    qm = jax.nn.silu(centred_conv(qc, conv_qk[:, :W])).astype(jnp.float32)
    km = jax.nn.silu(centred_conv(kc, conv_qk[:, W:])).astype(jnp.float32)
    to_heads = lambda t: t.astype(jnp.float32).reshape(B, L, H, dh).transpose(0, 2, 1, 3)
    qm, km, vm = to_heads(qm), to_heads(km), to_heads(vc)
    gates = gc.astype(jnp.float32).reshape(B, L, 2, 2, H) + gate_bias.astype(jnp.float32)
    gates = gates.transpose(2, 3, 0, 4, 1)
    h_fwd = mlstm_chunked(qm, km, vm, gates[0, 0], gates[0, 1])
    flip = lambda t: jnp.flip(t, axis=2)
    h_bwd = flip(mlstm_chunked(flip(qm), flip(km), flip(vm), flip(gates[1, 0]), flip(gates[1, 1])))
    hc = (h_fwd + h_bwd).transpose(0, 2, 1, 3)
    hc = jax.nn.sigmoid(oc.astype(jnp.float32)).reshape(B, L, H, dh) * hc
    mu = jnp.mean(hc, axis=-1, keepdims=True)
    var = jnp.mean(jnp.square(hc - mu), axis=-1, keepdims=True)
    yc = ((hc - mu) * lax.rsqrt(var + EPS) * head_gain.astype(jnp.float32).reshape(H, dh)).reshape(B, L, W)

    kv = (rms_norm(mem, mem_norm) @ w_mem_kv).reshape(B, N_MEM, 2, H, dh)
    qd4 = qd.reshape(B, L, H, dh)
    s = jnp.einsum('blhc,bmhc->bhlm', qd4, kv[:, :, 0]).astype(jnp.float32) * (dh ** -0.5)
    p = jax.nn.softmax(s, axis=-1)
    yd = jnp.einsum('bhlm,bmhc->blhc', p.astype(kv.dtype), kv[:, :, 1]).reshape(B, L, W)

    branches = jnp.stack([ya, yb, yc, yd], axis=2).astype(x.dtype)
    gated = branches * jax.nn.silu(gp).reshape(B, L, N_BRANCHES, W)
    proj = jnp.einsum('blgc,gcd->blgd', gated, w_branch)
    merged = jnp.einsum('blgd,blgd->bld', jax.nn.sigmoid(mg).reshape(B, L, N_BRANCHES, D_MODEL), proj)
    out = merged @ w_out
    return x + rms_norm(out, norm_post)


def encoder_trunk(x, mem, rel_bias, norm_pre, w_in, conv_qk, mlstm_gate_bias, mlstm_head_gain,
                  mem_norm, w_mem_kv, w_branch, w_out, norm_post):
    for layer in range(DEPTH):
        x = encoder_layer(x, mem, rel_bias, norm_pre[layer], w_in[layer], conv_qk[layer],
                          mlstm_gate_bias[layer], mlstm_head_gain[layer], mem_norm[layer],
                          w_mem_kv[layer], w_branch[layer], w_out[layer], norm_post[layer])
    return x


def setup_inputs(seed: int = 0) -> dict:
    key = jax.random.key(seed)
    ks = jax.random.split(key, 20)
    nrm = lambda k, shape, scale: jax.random.normal(k, shape, jnp.float32) * scale
    H = N_HEADS_BRANCH
    i_bias = nrm(ks[10], (DEPTH, 2, 1, H), 0.1)
    f_bias = jnp.linspace(3.0, 6.0, H, dtype=jnp.float32)[None, None, None, :] + nrm(ks[11], (DEPTH, 2, 1, H), 0.1)
    return {
        "x_prompt": nrm(ks[0], (BATCH, SEQ, D_MODEL), 1.0),
        "x_sample": nrm(ks[1], (DEC_BATCH, DEC_SEQ, D_MODEL), 1.0),
        "mem_prompt": nrm(ks[2], (BATCH, N_MEM, D_MODEL), 1.0),
        "mem_sample": nrm(ks[3], (DEC_BATCH, N_MEM, D_MODEL), 1.0),
        "rel_bias": nrm(ks[4], (REL_BUCKETS, A_HEADS_TOTAL), 0.5),
        "norm_pre": 1.0 + nrm(ks[5], (DEPTH, D_MODEL), 0.1),
        "w_in": nrm(ks[6], (DEPTH, D_MODEL, IN_WIDTH), D_MODEL ** -0.5),
        "conv_qk": nrm(ks[7], (DEPTH, MLSTM_CONV, 2 * BRANCH_WIDTH), MLSTM_CONV ** -0.5),
        "mlstm_gate_bias": jnp.concatenate([i_bias, f_bias], axis=2),
        "mlstm_head_gain": 1.0 + nrm(ks[8], (DEPTH, BRANCH_WIDTH), 0.1),
        "mem_norm": 1.0 + nrm(ks[9], (DEPTH, D_MODEL), 0.1),
        "w_mem_kv": nrm(ks[12], (DEPTH, D_MODEL, 2 * BRANCH_WIDTH), D_MODEL ** -0.5),
        "w_branch": nrm(ks[13], (DEPTH, N_BRANCHES, BRANCH_WIDTH, D_MODEL), BRANCH_WIDTH ** -0.5),
        "w_out": nrm(ks[14], (DEPTH, D_MODEL, D_MODEL), D_MODEL ** -0.5),
        "norm_post": 1.0 + nrm(ks[15], (DEPTH, D_MODEL), 0.1),
    }


def reference(x_prompt, x_sample, mem_prompt, mem_sample, rel_bias, norm_pre, w_in, conv_qk,
              mlstm_gate_bias, mlstm_head_gain, mem_norm, w_mem_kv, w_branch, w_out, norm_post):
    y_prompt = encoder_trunk(x_prompt, mem_prompt, rel_bias, norm_pre, w_in, conv_qk, mlstm_gate_bias,
                             mlstm_head_gain, mem_norm, w_mem_kv, w_branch, w_out, norm_post)
    y_sample = encoder_trunk(x_sample, mem_sample, rel_bias, norm_pre, w_in, conv_qk, mlstm_gate_bias,
                             mlstm_head_gain, mem_norm, w_mem_kv, w_branch, w_out, norm_post)
    return (y_prompt, y_sample)
```

```python
import numpy as np
from contextlib import ExitStack
import concourse.bass as bass
import concourse.mybir as mybir
from concourse.bass_utils import run_bass_kernel_spmd

F32 = mybir.dt.float32
BF16 = mybir.dt.bfloat16
AF = mybir.ActivationFunctionType
ALU = mybir.AluOpType
AX = mybir.AxisListType

DM = 1024
NIN = 13840
NEGM = -30000.0
EPS = 1e-6
O_QA, O_KA, O_VA, O_XB, O_QC, O_KC, O_VC, O_OC, O_GC, O_QD, O_GP, O_MG = (
    0, 1536, 3072, 4608, 5120, 5632, 6144, 6656, 7168, 7184, 7696, 9744)
DIL = (1, 4, 16)


class Reg:
    __slots__ = ("w", "rs")

    def __init__(self):
        self.w = None
        self.rs = []


class V:
    __slots__ = ("ap", "r")

    def __init__(self, ap, r):
        self.ap = ap
        self.r = r


class Buf:
    def __init__(self, h, r=None):
        self.h = h
        self.r = r if r is not None else Reg()

    def __getitem__(self, k):
        return V(self.h[k], self.r)

    def re(self, pat, **kw):
        return Buf(self.h.rearrange(pat, **kw), self.r)

    def sub(self, k):
        return Buf(self.h[k], self.r)


class Prog:
    NSLOT = 8
    BASE = {"pe": ("tensor", 1), "act": ("scalar", 1), "dve": ("vector", 1), "pool": ("gpsimd", 1)}
    DMAQ = {"sp": "sync", "gq": "gpsimd"}

    def __init__(self, nc, st):
        self.nc = nc
        self.STREAMS = dict(self.BASE)
        for q, phys in self.DMAQ.items():
            for i in range(self.NSLOT):
                self.STREAMS["%s%d" % (q, i)] = (phys, 16)
        self.phys = {p: [] for p in ("tensor", "scalar", "vector", "gpsimd", "sync")}
        self.cnt = {s: 0 for s in self.STREAMS}
        self.qcnt = {q: 0 for q in self.DMAQ}
        self.waited = {p: {s: 0 for s in self.STREAMS} for p in self.phys}
        self.sems = {s: st.enter_context(nc.semaphore("sem_" + s)) for s in self.STREAMS}
        self.nops = 0

    def op(self, stream, fn, reads=(), writes=()):
        deps = {}
        if stream in self.DMAQ:
            q = stream
            stream = "%s%d" % (q, self.qcnt[q] % self.NSLOT)
            self.qcnt[q] += 1
            if self.cnt[stream] > 0:
                deps[stream] = self.cnt[stream]
        phys = self.STREAMS[stream][0]

        def need(t):
            if t is not None and t[1] > deps.get(t[0], 0):
                deps[t[0]] = t[1]
        for r in reads:
            need(r.w)
        for r in writes:
            need(r.w)
            for t in r.rs:
                need(t)
        waits = []
        for s_, c_ in deps.items():
            if s_ == "pe" and stream == "pe":
                continue
            if c_ > self.waited[phys][s_]:
                self.waited[phys][s_] = c_
                waits.append((s_, c_))
        self.cnt[stream] += 1
        me = (stream, self.cnt[stream])
        for r in reads:
            r.rs.append(me)
            if len(r.rs) > 64:
                best = {}
                for t in r.rs:
                    if t[1] > best.get(t[0], 0):
                        best[t[0]] = t[1]
                r.rs = list(best.items())
        for r in writes:
            r.w = me
            r.rs = []
        self.phys[phys].append((stream, waits, fn))
        self.nops += 1

    def flush(self):
        final = [(s_, c_) for s_, c_ in self.cnt.items() if c_ > 0]
        for p in self.phys:
            w = [(s_, c_) for s_, c_ in final if c_ > self.waited[p][s_]]
            for s_, c_ in w:
                self.waited[p][s_] = c_
            self.phys[p].append((None, w, None))
        with self.nc.Block() as block:
            for phys, lst in self.phys.items():
                def body(engine, lst=lst):
                    for stream, waits, fn in lst:
                        for s_, c_ in waits:
                            engine.wait_ge(self.sems[s_], c_ * self.STREAMS[s_][1])
                        if fn is not None:
                            fn(engine).then_inc(self.sems[stream], self.STREAMS[stream][1])
                getattr(block, phys)(body)
        self.phys = {p: [] for p in self.phys}


class K:
    def __init__(self, nc, st):
        self.nc = nc
        self.P = Prog(nc, st)
        self.banks = [Buf(st.enter_context(nc.psum_tensor("psb%d" % i, [128, 512], F32))) for i in range(8)]
        self.bi = 0
        self.dq = 0

    def ps(self):
        b = self.banks[self.bi % 8]
        self.bi += 1
        return b

    def mm(self, out, lhsT, rhs, start=True, stop=True):
        self.P.op("pe", lambda e: e.matmul(out.ap, lhsT.ap, rhs.ap, start=start, stop=stop),
                  [lhsT.r, rhs.r], [out.r])

    def act(self, out, in_, func, bias=None, scale=None):
        kw = {}
        rd = [in_.r]
        if bias is not None:
            if isinstance(bias, V):
                kw["bias"] = bias.ap
                rd.append(bias.r)
            else:
                kw["bias"] = bias
        if scale is not None:
            if isinstance(scale, V):
                kw["scale"] = scale.ap
                rd.append(scale.r)
            else:
                kw["scale"] = scale
        self.P.op("act", lambda e: e.activation(out.ap, in_.ap, func, **kw), rd, [out.r])

    def _sc(self, s, rd):
        if isinstance(s, V):
            rd.append(s.r)
            return s.ap
        return s

    def tt(self, out, a, b, op, eng="dve"):
        self.P.op(eng, lambda e: e.tensor_tensor(out.ap, a.ap, b.ap, op), [a.r, b.r], [out.r])

    def ts(self, out, a, s1, op0, s2=None, op1=None, eng="dve"):
        rd = [a.r]
        s1 = self._sc(s1, rd)
        s2 = self._sc(s2, rd)
        if op1 is None:
            self.P.op(eng, lambda e: e.tensor_scalar(out.ap, a.ap, s1, None, op0), rd, [out.r])
        else:
            self.P.op(eng, lambda e: e.tensor_scalar(out.ap, a.ap, s1, s2, op0, op1), rd, [out.r])

    def stt(self, out, a, s, b, op0, op1, eng="dve"):
        rd = [a.r, b.r]
        s = self._sc(s, rd)
        self.P.op(eng, lambda e: e.scalar_tensor_tensor(out.ap, a.ap, s, b.ap, op0, op1), rd, [out.r])

    def copy(self, out, in_, eng="dve"):
        if eng == "act":
            self.P.op("act", lambda e: e.copy(out.ap, in_.ap), [in_.r], [out.r])
        else:
            self.P.op(eng, lambda e: e.tensor_copy(out.ap, in_.ap), [in_.r], [out.r])

    def memset(self, out, val, eng="dve"):
        self.P.op(eng, lambda e: e.memset(out.ap, val), [], [out.r])

    def recip(self, out, in_):
        self.P.op("dve", lambda e: e.reciprocal(out.ap, in_.ap), [in_.r], [out.r])

    def rsum(self, out, in_):
        self.P.op("dve", lambda e: e.reduce_sum(out.ap, in_.ap, AX.X), [in_.r], [out.r])

    def dma(self, out, in_, q="sp", slow=False):
        if slow:
            self.P.op(q, lambda e: e.dma_start(out=out.ap, in_=in_.ap, allow_slow_non_contiguous=True), [in_.r], [out.r])
        else:
            self.P.op(q, lambda e: e.dma_start(out=out.ap, in_=in_.ap), [in_.r], [out.r])


class Stage:
    def __init__(self, k):
        self.k = k

    def __enter__(self):
        self.st = ExitStack()
        self.n = 0
        return self

    def sb(self, shape, dt, name=None):
        self.n += 1
        K.uid = getattr(K, "uid", 0) + 1
        import os
        if os.environ.get("KDEBUG_SB"):
            print("sb", K.uid, shape, dt, "remaining", self.k.nc.sbuf_bytes_remaining)
        return Buf(self.st.enter_context(self.k.nc.sbuf_tensor("%s_%d" % (name or "t", K.uid), shape, dt)))

    def __exit__(self, *a):
        self.k.P.flush()
        self.st.close()
        return False


def t5_bucket(rel):
    nb = 16
    max_exact = 8
    ret = (rel > 0).astype(np.int32) * nb
    n = np.abs(rel)
    large = max_exact + (np.log(np.maximum(n, 1) / max_exact) / np.log(1024 / max_exact)
                         * (nb - max_exact)).astype(np.int32)
    large = np.minimum(large, nb - 1)
    return (ret + np.where(n < max_exact, n, large)).astype(np.int32)


def make_consts(seq_lens):
    c = {}
    c["ident"] = np.eye(128, dtype=np.float32)
    j = np.arange(128)
    ang = 2 * np.pi * np.outer(j, j) / 128.0
    C, S = np.cos(ang), np.sin(ang)
    c["cs_a"] = np.concatenate([C, -S], 1).astype(np.float32)
    c["cs_b"] = np.concatenate([S, C], 1).astype(np.float32)
    for L in sorted(set(seq_lens)):
        n2 = L // 128
        l2 = np.arange(n2)
        a = 2 * np.pi * np.outer(l2, j) / L
        tw = np.zeros((128, 4, 128), np.float32)
        tw[:n2, 0] = np.cos(a); tw[:n2, 1] = -np.sin(a)
        tw[:n2, 2] = np.sin(a); tw[:n2, 3] = np.cos(a)
        c["tw%d" % L] = tw
        a3 = 2 * np.pi * np.outer(l2, l2) / n2
        sc = 1.0 / np.sqrt(L * 128.0)
        m3 = np.zeros((128, 2, 128), np.float32)
        m3[:n2, 0, :n2] = np.cos(a3) * sc
        m3[:n2, 1, :n2] = np.sin(a3) * sc
        c["m3_%d" % L] = m3
    p = np.arange(128)[:, None]
    i = np.arange(256)[None, :]
    band = (p <= i) & (i <= p + 128)
    c["bandmask"] = np.where(band, 0.0, NEGM).astype(np.float32)
    u = np.arange(128)[:, None]
    t = np.arange(128)[None, :]
    tri = np.zeros((128, 5, 128), np.float32)
    tri[:, 0] = (u <= t); tri[:, 1] = (u > t); tri[:, 2] = 1.0; tri[:, 3] = (u >= t); tri[:, 4] = (u < t)
    c["tri"] = tri
    return c


def bias_idx():
    p = np.arange(128)[:, None]
    i = np.arange(256)[None, :]
    rel = np.clip(p + 64 - i, -64, 64)
    return np.stack([t5_bucket(d * rel) for d in DIL])


def build(seq_lens, depth, debug=False, run_layers=None):
    nc = bass.Bass("TRN2", target_bir_lowering=False)
    T = sum(seq_lens)
    NS = len(seq_lens)
    sbase = [sum(seq_lens[:i]) for i in range(NS)]
    NT = T // 512
    dt_in = lambda name, shape: Buf(nc.dram_tensor(name, shape, F32, kind="ExternalInput").ap())
    xin = dt_in("xin", [T, DM])
    memin = dt_in("memin", [NS * 256, DM])
    w_in = dt_in("w_in", [depth, DM, NIN])
    w_kv = dt_in("w_mem_kv", [depth, DM, DM])
    w_br = dt_in("w_branch", [depth, 4 * 512, DM])
    w_out = dt_in("w_out", [depth, DM, DM])
    n_pre = dt_in("norm_pre", [depth, DM])
    n_post = dt_in("norm_post", [depth, DM])
    n_mem = dt_in("mem_norm", [depth, DM])
    conv = dt_in("conv_qk", [depth, 5, DM])
    gbias = dt_in("gate_bias", [depth, 16])
    hgain = dt_in("head_gain", [depth, 512])
    biasT = dt_in("biasT", [12, 128, 256])
    cst = make_consts(seq_lens)
    cin = {k: dt_in("c_" + k, list(v.shape)) for k, v in cst.items()}
    yout = Buf(nc.dram_tensor("yout", [T, DM], F32, kind="ExternalOutput").ap())
    dram = lambda name, shape, dt: Buf(nc.dram_tensor(name, shape, dt).ap())
    xT = dram("xT", [DM, T], F32)
    hT = dram("hT", [DM, T], BF16)
    qkT = dram("qkT", [4608, T], BF16)
    tokm = dram("tokm", [T, 3584], BF16)
    gts = dram("gts", [T, 16], F32)
    gpT = dram("gpT", [2048, T], BF16)
    mgT = dram("mgT", [4096, T], BF16)
    brT = dram("brT", [2048, T], BF16)
    hF = dram("hF", [T, 128], F32)
    hB = dram("hB", [T, 128], F32)
    dbg = {}
    if debug:
        dbg["brT"] = Buf(nc.dram_tensor("dbg_brT", [2048, T], BF16, kind="ExternalOutput").ap())

    with ExitStack() as st:
        k = K(nc, st)
        G = Stage(k)
        G.__enter__()
        ident = G.sb([128, 128], F32)
        identb = G.sb([128, 128], BF16)
        onesb = G.sb([128, 128], BF16)
        k.dma(ident[:], cin["ident"][:])
        k.copy(identb[:], ident[:])
        k.memset(onesb[:], 1.0)
        epsb = G.sb([128, 1], F32)
        k.memset(epsb[:], EPS)
        csa = G.sb([128, 256], F32); csb = G.sb([128, 256], F32)
        csab = G.sb([128, 256], BF16); csbb = G.sb([128, 256], BF16)
        k.dma(csa[:], cin["cs_a"][:]); k.dma(csb[:], cin["cs_b"][:])
        k.copy(csab[:], csa[:]); k.copy(csbb[:], csb[:])
        tri = G.sb([128, 5, 128], F32)
        k.dma(tri[:], cin["tri"][:])
        bm = G.sb([128, 12, 256], F32)
        bmask = G.sb([128, 256], F32)
        k.dma(bmask[:], cin["bandmask"][:])
        k.dma(bm[:], biasT.re("g p i -> p g i")[:])
        for g in range(12):
            k.tt(bm[:, g, :], bm[:, g, :], bmask[:], ALU.add)
        k.P.flush()

        with Stage(k) as S:
            xi = [S.sb([128, DM], F32) for _ in range(2)]
            xo = [S.sb([128, 8, 128], F32) for _ in range(2)]
            for tb in range(T // 128):
                a = xi[tb % 2]; o = xo[tb % 2]
                k.dma(a[:], xin[tb * 128:(tb + 1) * 128, :])
                for half in range(2):
                    pb = k.ps()
                    for c4 in range(4):
                        kc = half * 4 + c4
                        k.mm(pb[:, c4 * 128:(c4 + 1) * 128], a[:, kc * 128:(kc + 1) * 128], ident[:])
                    k.copy(o[:, half * 4:half * 4 + 4, :], pb.re("p (a b) -> p a b", a=4)[:], eng="dve" if half else "act")
                k.dma(xT.re("(kc p) t -> p kc t", p=128)[:, :, tb * 128:(tb + 1) * 128], o[:], q="gq")

        for layer in range(depth if run_layers is None else run_layers):
            build_layer(k, layer, depth, seq_lens, sbase, T, dict(locals()))

        with Stage(k) as S:
            xi = [S.sb([128, 8, 128], F32) for _ in range(2)]
            xo = [S.sb([128, DM], F32) for _ in range(2)]
            for tb in range(T // 128):
                a = xi[tb % 2]; o = xo[tb % 2]
                k.dma(a[:], xT.re("(kc p) t -> p kc t", p=128)[:, :, tb * 128:(tb + 1) * 128])
                for half in range(2):
                    pb = k.ps()
                    for c4 in range(4):
                        kc = half * 4 + c4
                        k.mm(pb[:, c4 * 128:(c4 + 1) * 128], a[:, kc, :], ident[:])
                    k.copy(o[:, half * 512:(half + 1) * 512], pb[:], eng="dve" if half else "act")
                k.dma(yout[tb * 128:(tb + 1) * 128, :], o[:], q="gq")
            if debug:
                k.dma(dbg["brT"][:], brT[:], q="gq")
        G.k.P.flush()
        G.st.close()
        print('total ops', k.P.nops, k.P.cnt)
    return nc, cst


def build_layer(k, layer, depth, seq_lens, sbase, T, c):
    import os
    en = os.environ.get("KSTAGES", "01DBAC3")
    if "0" in en:
        stage0_norm(k, layer, T, c)
    if "1" in en:
        stage1_proj(k, layer, T, c)
    for si, L in enumerate(seq_lens):
        if "D" in en:
            stageD(k, layer, si, sbase[si], L, c)
        if "B" in en:
            stageB(k, layer, si, sbase[si], L, c)
        if "A" in en:
            stageA(k, layer, si, sbase[si], L, c)
        if "C" in en:
            stageC(k, layer, si, sbase[si], L, c)
    if "3" in en:
        stage3_out(k, layer, T, c)


def zero_rows(k, c, r0, sb, L):
    with Stage(k) as S:
        z = S.sb([128, 4, 1024], BF16)
        k.memset(z[:], 0.0)
        for t in range(L // 1024):
            k.dma(fm(c["brT"].sub(slice(r0, r0 + 512)))[:, :, sb + t * 1024: sb + (t + 1) * 1024], z[:], q="gq")


def ssl(start, count, step):
    return slice(start, start + (count - 1) * step + 1, step)


def fm(buf):
    return buf.re("(kc p) t -> p kc t", p=128)


def stage0_norm(k, layer, T, c):
    with Stage(k) as S:
        gpre = S.sb([128, 8], F32)
        k.dma(gpre[:], c["n_pre"].sub(layer).re("(kc p) -> p kc", p=128)[:], slow=True)
        xs = [S.sb([128, 8, 512], F32) for _ in range(2)]
        sq = S.sb([128, 8, 512], BF16)
        hs = [S.sb([128, 8, 512], BF16) for _ in range(2)]
        rstd = S.sb([128, 512], F32)
        for t in range(T // 512):
            x = xs[t % 2]; h = hs[t % 2]
            sl = slice(t * 512, (t + 1) * 512)
            k.dma(x[:], fm(c["xT"])[:, :, sl])
            k.act(sq[:], x[:], AF.Square)
            pb = k.ps()
            for kc in range(8):
                k.mm(pb[:], c["onesb"][:], sq[:, kc, :], start=(kc == 0), stop=(kc == 7))
            k.act(rstd[:], pb[:], AF.Sqrt, bias=c["epsb"][:, 0:1], scale=1.0 / DM)
            k.recip(rstd[:], rstd[:])
            for kc in range(8):
                k.stt(h[:, kc, :], x[:, kc, :], gpre[:, kc:kc + 1], rstd[:], ALU.mult, ALU.mult)
            k.dma(fm(c["hT"])[:, :, sl], h[:], q="gq")


def stage1_proj(k, layer, T, c):
    w_in = c["w_in"].sub(layer).re("(kc p) n -> p kc n", p=128)
    NT = T // 512
    with Stage(k) as S:
        wA = S.sb([128, 8, 3600], BF16)
        k.dma(wA[:, :, 0:1536], w_in[:, :, O_VA:O_VA + 1536], q="gq")
        k.dma(wA[:, :, 2560:3584], w_in[:, :, O_VC:O_VC + 1024], q="gq")
        k.dma(wA[:, :, 3584:3600], w_in[:, :, O_GC:O_GC + 16], q="gq")
        wxb = S.sb([128, 8, 512], F32)
        wxT = S.sb([128, 4, 1024], F32)
        k.dma(wxb[:], w_in[:, :, O_XB:O_XB + 512])
        for g in range(4):
            for half in range(2):
                pb = k.ps()
                for c4 in range(4):
                    kc = half * 4 + c4
                    k.mm(pb[:, c4 * 128:(c4 + 1) * 128], wxb[:, kc, g * 128:(g + 1) * 128], c["ident"][:])
                k.copy(wxT[:, g, half * 512:(half + 1) * 512], pb[:])
        for g in range(4):
            for kc in range(8):
                pb = k.ps()
                k.mm(pb[:, 0:256], wxT[:, g, kc * 128:(kc + 1) * 128], c["csa"][:])
                k.copy(wA[:, kc, 1536 + g * 256:1536 + (g + 1) * 256], pb[:, 0:256], eng="act" if kc % 2 else "dve")
        hs = [S.sb([128, 8, 512], BF16) for _ in range(2)]
        os_ = [S.sb([128, 4, 3584], BF16) for _ in range(2)]
        og = [S.sb([128, 4, 16], F32) for _ in range(2)]
        ev = 0
        for t in range(NT):
            h = hs[t % 2]; o = os_[t % 2]; g_ = og[t % 2]
            sl = slice(t * 512, (t + 1) * 512)
            k.dma(h[:], fm(c["hT"])[:, :, sl])
            for tb in range(4):
                for cc in range(8):
                    c0 = cc * 512
                    w = 512 if cc < 7 else 16
                    pb = k.ps()
                    for kc in range(8):
                        k.mm(pb[:, 0:w], h[:, kc, tb * 128:(tb + 1) * 128], wA[:, kc, c0:c0 + w],
                             start=(kc == 0), stop=(kc == 7))
                    if cc == 7:
                        k.copy(g_[:, tb, :], pb[:, 0:16])
                    elif cc == 6:
                        k.act(o[:, tb, c0:c0 + 512], pb[:], AF.Sigmoid)
                    else:
                        k.copy(o[:, tb, c0:c0 + 512], pb[:], eng="act" if ev % 2 else "dve")
                        ev += 1
            k.dma(c["tokm"].sub(sl).re("(tb p) n -> p tb n", p=128)[:], o[:], q="gq")
            k.dma(c["gts"].sub(sl).re("(tb p) n -> p tb n", p=128)[:], g_[:], q="gq")

    def fm_pass(specs):
        ncols = sum(s[1] for s in specs)
        nb = ncols // 128
        with Stage(k) as S:
            w = S.sb([128, 8, ncols], BF16)
            blocks = []
            o0 = 0
            for (wc, n, dest, r0, func, scale) in specs:
                k.dma(w[:, :, o0:o0 + n], w_in[:, :, wc:wc + n], q="gq")
                for b in range(n // 128):
                    blocks.append((o0 + b * 128, func, scale))
                o0 += n
            hs = [S.sb([128, 8, 512], BF16) for _ in range(2)]
            os_ = [S.sb([128, nb, 512], BF16) for _ in range(2)]
            ev = 0
            for t in range(NT):
                h = hs[t % 2]; o = os_[t % 2]
                sl = slice(t * 512, (t + 1) * 512)
                k.dma(h[:], fm(c["hT"])[:, :, sl])
                for bi, (wc, func, scale) in enumerate(blocks):
                    pb = k.ps()
                    for kc in range(8):
                        k.mm(pb[:], w[:, kc, wc:wc + 128], h[:, kc, :], start=(kc == 0), stop=(kc == 7))
                    if func is not None:
                        k.act(o[:, bi, :], pb[:], func)
                    elif ev % 2:
                        k.act(o[:, bi, :], pb[:], AF.Copy, scale=scale)
                        ev += 1
                    else:
                        k.ts(o[:, bi, :], pb[:], scale, ALU.mult)
                        ev += 1
                o0 = 0
                for (wc, n, dest, r0, func, scale) in specs:
                    k.dma(fm(dest.sub(slice(r0, r0 + n)))[:, :, sl], o[:, o0 // 128:(o0 + n) // 128, :], q="gq")
                    o0 += n
    sq = 128.0 ** -0.5
    qk = c["qkT"]
    fm_pass([(O_QA, 1536, qk, 0, None, sq), (O_KA, 1536, qk, 1536, None, 1.0), (O_QC, 1024, qk, 3072, None, 1.0),
             (O_QD, 512, qk, 4096, None, sq)])
    fm_pass([(O_GP, 2048, c["gpT"], 0, AF.Silu, 1.0), (O_MG, 1024, c["mgT"], 0, AF.Sigmoid, 1.0)])
    fm_pass([(O_MG + 1024, 3072, c["mgT"], 1024, AF.Sigmoid, 1.0)])


def stageD(k, layer, si, sb, L, c):
    with Stage(k) as S:
        wkv = S.sb([128, 8, 1024], BF16)
        k.dma(wkv[:], c["w_kv"].sub(layer).re("(kc p) n -> p kc n", p=128)[:], q="gq")
        gm = S.sb([128, DM], F32)
        k.dma(gm[:], V(c["n_mem"].h[layer:layer + 1, :].broadcast_to([128, DM]), c["n_mem"].r))
        mhT = S.sb([128, 8, 256], BF16)
        for mb in range(2):
            m = S.sb([128, DM], F32)
            sqt = S.sb([128, DM], F32)
            ss = S.sb([128, 1], F32)
            k.dma(m[:], c["memin"][si * 256 + mb * 128: si * 256 + (mb + 1) * 128, :])
            k.act(sqt[:], m[:], AF.Square)
            k.rsum(ss[:], sqt[:])
            k.act(ss[:], ss[:], AF.Sqrt, bias=c["epsb"][:, 0:1], scale=1.0 / DM)
            k.recip(ss[:], ss[:])
            k.stt(m[:], m[:], ss[:, 0:1], gm[:], ALU.mult, ALU.mult)
            for half in range(2):
                pb = k.ps()
                for c4 in range(4):
                    kc = half * 4 + c4
                    k.mm(pb[:, c4 * 128:(c4 + 1) * 128], m[:, kc * 128:(kc + 1) * 128], c["ident"][:])
                k.copy(mhT[:, half * 4:half * 4 + 4, mb * 128:(mb + 1) * 128], pb.re("p (a b) -> p a b", a=4)[:])
        KT = S.sb([128, 4, 256], BF16)
        Vd = S.sb([128, 2, 512], BF16)
        for h in range(4):
            pb = k.ps()
            for kc in range(8):
                k.mm(pb[:, 0:256], wkv[:, kc, h * 128:(h + 1) * 128], mhT[:, kc, :], start=(kc == 0), stop=(kc == 7))
            k.copy(KT[:, h, :], pb[:, 0:256])
        for mb in range(2):
            pb = k.ps()
            for kc in range(8):
                k.mm(pb[:], mhT[:, kc, mb * 128:(mb + 1) * 128], wkv[:, kc, 512:1024], start=(kc == 0), stop=(kc == 7))
            k.copy(Vd[:, mb, :], pb[:])
        qs = [S.sb([128, 4, 512], BF16) for _ in range(2)]
        os_ = [S.sb([128, 4, 512], BF16) for _ in range(2)]
        Es = [S.sb([128, 2, 512], BF16) for _ in range(2)]
        rd = S.sb([128, 512], F32)
        qd = fm(c["qkT"].sub(slice(4096, 4608)))
        yd = fm(c["brT"].sub(slice(1536, 2048)))
        it = 0
        for t in range(L // 512):
            q = qs[t % 2]; o = os_[t % 2]
            sl = slice(sb + t * 512, sb + (t + 1) * 512)
            k.dma(q[:], qd[:, :, sl])
            for h in range(4):
                E = Es[it % 2]; it += 1
                for mb in range(2):
                    pb = k.ps()
                    k.mm(pb[:], KT[:, h, mb * 128:(mb + 1) * 128], q[:, h, :])
                    k.act(E[:, mb, :], pb[:], AF.Exp)
                po = k.ps(); pd = k.ps()
                for mb in range(2):
                    k.mm(po[:], Vd[:, mb, h * 128:(h + 1) * 128], E[:, mb, :], start=(mb == 0), stop=(mb == 1))
                for mb in range(2):
                    k.mm(pd[:], c["onesb"][:], E[:, mb, :], start=(mb == 0), stop=(mb == 1))
                k.recip(rd[:], pd[:])
                k.tt(o[:, h, :], po[:], rd[:], ALU.mult)
            k.dma(yd[:, :, sl], o[:], q="gq")


def stageA(k, layer, si, sb, L, c):
    QR = 4096
    qk = c["qkT"]; tokm = c["tokm"]; bm = c["bm"]
    with Stage(k) as S:
        qbs = [S.sb([128, QR], BF16) for _ in range(2)]
        kbs = [S.sb([128, QR + 4096], BF16) for _ in range(2)]
        vbs = [S.sb([128, (QR + 4096) // 128, 128], BF16) for _ in range(2)]
        nd = S.sb([128, 2, QR], F32)
        yas = [S.sb([128, QR], BF16) for _ in range(2)]
        PTs = [S.sb([128, 256], BF16) for _ in range(12)]
        tmps = [S.sb([128, 256], F32) for _ in range(8)]
        it = 0; pi = 0; ri = 0
        for h in range(4):
            for Q0 in range(0, L, QR):
                Q1 = Q0 + QR
                for gi, d in enumerate(DIL):
                    M = L // d; NB = M // 128
                    KP0 = max(0, Q0 - 128 * d); KP1 = min(L, Q1 + 128 * d)
                    j0 = KP0 // (128 * d); j1 = KP1 // (128 * d); NBL = j1 - j0
                    qb = qbs[it % 2]; kb = kbs[it % 2]; vb = vbs[it % 2]; it += 1
                    qrow = gi * 512 + h * 128; krow = 1536 + gi * 512 + h * 128
                    k.dma(qb[:, 0:QR], qk[qrow:qrow + 128, sb + Q0:sb + Q1])
                    k.dma(kb[:, 0:KP1 - KP0], qk[krow:krow + 128, sb + KP0:sb + KP1])
                    vcol = gi * 512 + h * 128
                    for r in range(d):
                        src = tokm.sub((slice(sb + KP0 + r, sb + KP1, d), slice(vcol, vcol + 128)))
                        k.dma(vb[:, r * NBL:(r + 1) * NBL, :], src.re("(j p) n -> p j n", p=128)[:])
                    g12 = gi * 4 + h
                    a0 = Q0 // d; a1 = Q1 // d
                    jlo = max(0, a0 // 128 - 1); jhi = min(NB - 1, a1 // 128)
                    pts = []; segs = []
                    for r in range(d):
                        for j in range(jlo, jhi + 1):
                            w0 = 128 * j - 64
                            pts.append((r, j, max(0, a0 - w0), min(256, a1 - w0)))
                        for n in range(jlo, jhi + 2):
                            s0 = max(128 * n - 64, a0); s1 = min(128 * n + 64, a1)
                            if s1 <= s0:
                                continue
                            parts = []
                            if jlo <= n - 1 <= jhi:
                                parts.append((n - 1, 192 - 128 * n))
                            if jlo <= n <= jhi:
                                parts.append((n, 64 - 128 * n))
                            segs.append((r, n, s0, s1, parts))
                    done = {}
                    si_ = 0
                    GW = 4
                    for w_ in range(0, len(pts), GW):
                        wave = pts[w_:w_ + GW]
                        banks = {}
                        for ix, (r, j, c0, c1) in enumerate(wave):
                            if ix % 2 == 0:
                                cur = k.ps()
                            banks[ix] = (cur, (ix % 2) * 256)
                            ks = r + d * 128 * j - KP0
                            qs = r + d * (128 * j - 64 + c0) - Q0
                            k.mm(cur[:, (ix % 2) * 256 + c0:(ix % 2) * 256 + c1], kb[:, ssl(ks, 128, d)], qb[:, ssl(qs, c1 - c0, d)])
                        for ix, (r, j, c0, c1) in enumerate(wave):
                            cur, o_ = banks[ix]
                            tmp = tmps[(pi + ix) % len(tmps)]
                            k.tt(tmp[:, c0:c1], cur[:, o_ + c0:o_ + c1], bm[:, g12, c0:c1], ALU.add)
                        for ix, (r, j, c0, c1) in enumerate(wave):
                            tmp = tmps[(pi + ix) % len(tmps)]; P_ = PTs[(pi + ix) % len(PTs)]
                            k.act(P_[:, c0:c1], tmp[:, c0:c1], AF.Exp)
                            done[(r, j)] = P_
                        pi += len(wave)
                        ready = []
                        while si_ < len(segs) and all((segs[si_][0], jj) in done for jj, _ in segs[si_][4]):
                            ready.append(segs[si_]); si_ += 1
                        pbs = []
                        for (r, n, s0, s1, parts) in ready:
                            w = s1 - s0
                            pb2 = k.ps(); pbs.append(pb2)
                            for col0, use_v in ((0, True), (128, False)):
                                for ix, (jj, off) in enumerate(parts):
                                    lhs = vb[:, r * NBL + (jj - j0), :] if use_v else c["onesb"][:]
                                    k.mm(pb2[:, col0:col0 + w], lhs, done[(r, jj)][:, s0 + off:s1 + off],
                                         start=(ix == 0), stop=(ix == len(parts) - 1))
                        for (r, n, s0, s1, parts), pb2 in zip(ready, pbs):
                            w = s1 - s0
                            ls = r + d * s0 - Q0
                            dst = nd[:, :, ssl(ls, w, d)]
                            srcv = pb2.re("p (a n) -> p a n", a=4)[:, 0:2, 0:w]
                            if gi == 0:
                                k.copy(dst, srcv, eng="act")
                            else:
                                k.tt(dst, dst, srcv, ALU.add)
                    assert si_ == len(segs)
                ya = yas[ri % 2]; ri += 1
                k.recip(nd[:, 1, :], nd[:, 1, :])
                k.tt(ya[:], nd[:, 0, :], nd[:, 1, :], ALU.mult)
                k.dma(c["brT"][h * 128:(h + 1) * 128, sb + Q0:sb + Q1], ya[:], q="gq")


def stageB(k, layer, si, sb, L, c):
    N2 = L // 128
    with Stage(k) as S:
        tw = S.sb([128, 4, 128], F32)
        m3 = S.sb([128, 2, 128], F32)
        m3b = S.sb([128, 2, 128], BF16)
        k.dma(tw[:], c["cin"]["tw%d" % L][:])
        k.dma(m3[:], c["cin"]["m3_%d" % L][:])
        k.copy(m3b[:], m3[:])
        nbuf = 2 if N2 <= 64 else 1
        Zss = [S.sb([128, N2, 256], BF16) for _ in range(nbuf)]
        Zss = Zss * (2 // nbuf)
        Ap = S.sb([128, 128, 256], BF16)
        Ybs = [S.sb([128, 128, 128], BF16) for _ in range(nbuf)]
        Ybs = Ybs * (2 // nbuf)
        tAs = [S.sb([128, 2, 2, 128], F32) for _ in range(2)]
        tBs = [S.sb([128, 2, 2, 128], F32) for _ in range(2)]
        for fg in range(4):
            Zs = Zss[fg % 2]; Yb = Ybs[fg % 2]
            src = c["tokm"].sub((slice(sb, sb + L), slice(1536 + fg * 256, 1536 + (fg + 1) * 256)))
            k.dma(Zs[:], src.re("(l1 l2) n -> l1 l2 n", l2=N2)[:])
            for cp in range(64):
                pb = k.ps()
                for cc in range(2):
                    ch = 2 * cp + cc
                    k.mm(pb[0:N2, cc * 256:(cc + 1) * 256], Zs[:, :, ch], c["csab"][:], start=True, stop=False)
                    k.mm(pb[0:N2, cc * 256:(cc + 1) * 256], Zs[:, :, 128 + ch], c["csbb"][:], start=False, stop=True)
                pv = pb.re("p (a b n) -> p a b n", a=2, b=2)
                tA = tAs[cp % 2]; tB = tBs[cp % 2]
                shp = [N2, 2, 2, 128]
                bc = lambda buf, ap: V(ap.broadcast_to(shp), buf.r)
                k.tt(tA[0:N2], bc(pb, pv.h[0:N2, :, 0, :].unsqueeze(2)), bc(tw, tw.h[0:N2, 0:2, :].unsqueeze(1)), ALU.mult)
                k.tt(tB[0:N2], bc(pb, pv.h[0:N2, :, 1, :].unsqueeze(2)), bc(tw, tw.h[0:N2, 2:4, :].unsqueeze(1)), ALU.mult)
                k.tt(Ap.re("p c (b n) -> p c b n", b=2)[0:N2, 2 * cp:2 * cp + 2, :, :], tA[0:N2], tB[0:N2], ALU.add)
            for c4 in range(32):
                pb = k.ps()
                pv = pb.re("p (a n) -> p a n", a=4)
                k.mm(pv[0:N2], m3b[0:N2, 0, 0:N2], Ap[0:N2, 4 * c4:4 * c4 + 4, 0:128], start=True, stop=False)
                k.mm(pv[0:N2], m3b[0:N2, 1, 0:N2], Ap[0:N2, 4 * c4:4 * c4 + 4, 128:256], start=False, stop=True)
                k.copy(Yb[0:N2, 4 * c4:4 * c4 + 4, :], pv[0:N2], eng="act" if c4 % 2 else "dve")
            dst = c["brT"].sub((slice(512 + fg * 128, 512 + (fg + 1) * 128), slice(sb, sb + L)))
            dv = dst.re("c (k2 k1) -> k2 c k1", k1=128)
            for cb in range(8):
                k.dma(dv[:, cb * 16:(cb + 1) * 16, :], Yb[0:N2, cb * 16:(cb + 1) * 16, :], q="gq")


def stageC(k, layer, si, sb, L, c):
    NCH = L // 128
    CW = 1024
    tri = c["tri"]; qk = c["qkT"]; tokm = c["tokm"]
    with Stage(k) as S:
        gb = S.sb([128, 16], F32)
        gain = S.sb([128, 512], F32)
        cw = S.sb([128, 8, 5], F32)
        one1 = S.sb([128, 1], F32)
        k.memset(one1[:], 1.0)
        k.dma(gb[:], V(c["gbias"].h[layer:layer + 1, :].broadcast_to([128, 16]), c["gbias"].r))
        k.dma(gain[:], V(c["hgain"].h[layer:layer + 1, :].broadcast_to([128, 512]), c["hgain"].r))
        for blk in range(8):
            k.dma(cw[:, blk, :], c["conv"].sub((layer, slice(0, 5), slice(blk * 128, (blk + 1) * 128))).re("j p -> p j")[:], slow=True)
        g_sb = S.sb([128, NCH, 16], F32)
        k.dma(g_sb[:], c["gts"].sub(slice(sb, sb + L)).re("(n p) g -> p n g", p=128)[:])
        k.tt(g_sb[:], g_sb[:], V(gb.h[:].unsqueeze(1).broadcast_to([128, NCH, 16]), gb.r), ALU.add)
        qT = S.sb([128, L], BF16); kT = S.sb([128, L], BF16)
        v_sb = S.sb([128, NCH, 129], BF16)
        lf = S.sb([128, 2, NCH], F32); e1 = S.sb([128, NCH], F32); tw_ = S.sb([128, NCH], F32)
        EB = S.sb([128, 2, NCH], F32); WA = S.sb([128, 2, NCH], F32); SG = S.sb([128, 2, NCH], F32)
        raws = [S.sb([128, CW + 4], BF16) for _ in range(2)]
        accs = [S.sb([128, 512], F32) for _ in range(2)]
        dgs = [S.sb([128, 5, 128], BF16) for _ in range(2)]
        ai = 0
        mk2 = lambda shape, dt: [[S.sb(shape, dt) for _ in range(2)] for _ in range(2)]
        Cst = [S.sb([128, 129], F32) for _ in range(2)]
        CbfS = [[S.sb([128, 129], BF16) for _ in range(8)] for _ in range(2)]
        def mk1(shape, dt):
            one = [S.sb(shape, dt) for _ in range(2)]
            return [one, one]
        SmT = mk2([128, 4, 128], F32); At = mk1([128, 4, 128], F32); Et = At
        sqkT = mk2([128, 4, 128], BF16); KwT = mk2([128, 4, 128], BF16)
        intraT = mk2([128, 4, 129], F32); clocT = mk2([128, 4, 129], F32); totT = mk1([128, 4, 129], F32)
        dabT = mk2([128, 4], F32); hhT = mk1([128, 4, 128], F32)
        GC = 4
        hfl = [S.sb([128, GC, 128], F32) for _ in range(2)]
        hbl = [S.sb([128, GC, 128], F32)] * 2
        obT = [S.sb([128, GC, 128], BF16)] * 2
        s1T = [S.sb([128, GC], F32) for _ in range(2)]
        s2T = [S.sb([128, GC], F32) for _ in range(2)]
        sqtT = [S.sb([128, GC, 128], F32)] * 2
        ybT = [S.sb([128, GC, 128], BF16) for _ in range(2)]
        ycs = [S.sb([128, 2048], BF16)] * 2
        ci = 0
        for h in range(4):
            for dr in range(2):
                fcol = g_sb[:, :, dr * 8 + 4 + h]
                k.act(e1[:], fcol, AF.Exp, scale=-1.0)
                k.act(lf[:, dr, :], e1[:], AF.Ln, bias=one1[:, 0:1], scale=1.0)
            for dr in range(2):
                icol = g_sb[:, :, dr * 8 + h]
                pb = k.ps()
                k.mm(pb[:, 0:NCH], tri[:, 0 if dr == 0 else 3, :], lf[:, dr, :])
                k.act(EB[:, dr, :], pb[:, 0:NCH], AF.Exp, scale=-1.0)
                pb = k.ps()
                k.mm(pb[:, 0:NCH], tri[:, 1 if dr == 0 else 4, :], lf[:, dr, :])
                k.stt(tw_[:], pb[:, 0:NCH], -1.0, icol, ALU.mult, ALU.add)
                k.act(WA[:, dr, :], tw_[:], AF.Exp)
                pb = k.ps()
                k.mm(pb[:, 0:NCH], tri[:, 2, :], lf[:, dr, :])
                k.act(SG[:, dr, :], pb[:, 0:NCH], AF.Exp, scale=-1.0)
            for which in range(2):
                row = 3072 + which * 512 + h * 128
                blk = which * 4 + h
                dg = dgs[which]
                for j in range(5):
                    k.ts(dg[:, j, :], c["identb"][:], cw[:, blk, j:j + 1], ALU.mult)
                for c0 in range(0, L, CW):
                    raw = raws[ci % 2]; ci += 1
                    lo = max(0, c0 - 2); hi = min(L, c0 + CW + 2)
                    if c0 == 0:
                        k.memset(raw[:, 0:2], 0.0)
                    if c0 + CW >= L:
                        k.memset(raw[:, CW + 2:CW + 4], 0.0)
                    k.dma(raw[:, lo - (c0 - 2):hi - (c0 - 2)], qk[row:row + 128, sb + lo:sb + hi])
                    for hf in range(CW // 512):
                        pb = k.ps()
                        for j in range(5):
                            k.mm(pb[:], dg[:, j, :], raw[:, hf * 512 + j:hf * 512 + j + 512], start=(j == 0), stop=(j == 4))
                        cols = slice(c0 + hf * 512, c0 + (hf + 1) * 512)
                        if which == 0:
                            acc = accs[ai % 2]; ai += 1
                            k.act(acc[:], pb[:], AF.Silu)
                            k.ts(qT[:, cols], acc[:], 128.0 ** -0.5, ALU.mult)
                        else:
                            k.act(kT[:, cols], pb[:], AF.Silu)
            vsrc = tokm.sub((slice(sb, sb + L), slice(2560 + h * 128, 2560 + (h + 1) * 128)))
            k.dma(v_sb[:, :, 0:128], vsrc.re("(n p) n2 -> p n n2", p=128)[:])
            k.memset(v_sb[:, :, 128:129], 1.0)
            osrc = tokm.sub((slice(sb, sb + L), slice(3072 + h * 128, 3072 + (h + 1) * 128)))
            hD = [c["hF"].sub(slice(sb, sb + L)), c["hB"].sub(slice(sb, sb + L))]
            for dr in range(2):
                k.memset(Cst[dr][:], 0.0)
                k.memset(CbfS[dr][7][:], 0.0)
            mS = (0, 3)
            mA = (1, 4)
            mR = (0, 3)
            r2 = lambda bank: bank.re("p (a n) -> p a n", a=2)
            r4 = lambda bank: bank.re("p (a n) -> p a n", a=4)
            def gctx(g):
                lo = (4 * g, NCH - 4 - 4 * g)
                chunk = lambda dr, j: lo[dr] + j
                tslf = lambda dr, j: slice(chunk(dr, j) * 128, (chunk(dr, j) + 1) * 128)
                return g % 2, lo, chunk, tslf

            def phase1(g):
                gb, lo, chunk, tslf = gctx(g)
                bS = [k.ps(), k.ps()]
                for dr in range(2):
                    for j in range(4):
                        k.mm(bS[dr][:, j * 128:(j + 1) * 128], kT[:, tslf(dr, j)], qT[:, tslf(dr, j)])
                bK = [k.ps(), k.ps()]
                for dr in range(2):
                    for j in range(4):
                        k.mm(bK[dr][:, j * 128:(j + 1) * 128], kT[:, tslf(dr, j)], c["identb"][:])
                for dr in range(2):
                    for j in range(4):
                        n = chunk(dr, j)
                        k.act(At[gb][dr][:, j, :], tri[:, mA[dr], :], AF.Copy, scale=lf[:, dr, n:n + 1])
                for dr in range(2):
                    k.tt(SmT[gb][dr][:], r4(bS[dr])[:], V(tri.h[:, mS[dr], :].unsqueeze(1).broadcast_to([128, 4, 128]), tri.r), ALU.mult)
                for dr in range(2):
                    for j in range(4):
                        n = chunk(dr, j)
                        k.act(KwT[gb][dr][:, j, :], bK[dr][:, j * 128:(j + 1) * 128], AF.Copy, scale=WA[:, dr, n:n + 1])
                bD = [k.ps(), k.ps()]
                for dr in range(2):
                    for j in range(4):
                        k.mm(bD[dr][:, j * 128:(j + 1) * 128], At[gb][dr][:, j, :], tri[:, mR[dr], :])
                for dr in range(2):
                    for j in range(4):
                        n = chunk(dr, j)
                        k.act(Et[gb][dr][:, j, :], bD[dr][:, j * 128:(j + 1) * 128], AF.Exp,
                              bias=g_sb[:, n, dr * 8 + h:dr * 8 + h + 1], scale=-1.0)
                bC = [[k.ps(), k.ps()], [k.ps(), k.ps()]]
                for dr in range(2):
                    for j in range(4):
                        k.mm(bC[dr][j // 2][:, (j % 2) * 256:(j % 2) * 256 + 129], KwT[gb][dr][:, j, :], v_sb[:, chunk(dr, j), :])
                for dr in range(2):
                    k.tt(sqkT[gb][dr][:], Et[gb][dr][:], SmT[gb][dr][:], ALU.mult)
                for dr in range(2):
                    for hf in range(2):
                        k.copy(clocT[gb][dr][:, 2 * hf:2 * hf + 2, :], r2(bC[dr][hf])[:, :, 0:129])
                bI = [[k.ps(), k.ps()], [k.ps(), k.ps()]]
                for dr in range(2):
                    for j in range(4):
                        k.mm(bI[dr][j // 2][:, (j % 2) * 256:(j % 2) * 256 + 129], sqkT[gb][dr][:, j, :], v_sb[:, chunk(dr, j), :])
                for dr in range(2):
                    for hf in range(2):
                        k.copy(intraT[gb][dr][:, 2 * hf:2 * hf + 2, :], r2(bI[dr][hf])[:, :, 0:129], eng="act")

            def phase2(g):
                gb, lo, chunk, tslf = gctx(g)
                for pos in range(4):
                    for dr in range(2):
                        j = pos if dr == 0 else 3 - pos
                        n = chunk(dr, j)
                        st_ = 4 * g + pos
                        k.stt(Cst[dr][:], Cst[dr][:], SG[:, dr, n:n + 1], clocT[gb][dr][:, j, :], ALU.mult, ALU.add)
                        k.copy(CbfS[dr][st_ % 8][:], Cst[dr][:], eng="act")
                bX = [[k.ps(), k.ps()], [k.ps(), k.ps()]]
                for dr in range(2):
                    for pos in range(4):
                        j = pos if dr == 0 else 3 - pos
                        st_ = 4 * g + pos
                        k.mm(bX[dr][j // 2][:, (j % 2) * 256:(j % 2) * 256 + 129], qT[:, tslf(dr, j)], CbfS[dr][(st_ - 1) % 8][:])
                for dr in range(2):
                    for j in range(4):
                        n = chunk(dr, j)
                        k.stt(totT[gb][dr][:, j, :], bX[dr][j // 2][:, (j % 2) * 256:(j % 2) * 256 + 129], EB[:, dr, n:n + 1],
                              intraT[gb][dr][:, j, :], ALU.mult, ALU.add)
                for dr in range(2):
                    k.act(dabT[gb][dr][:], totT[gb][dr][:, :, 128], AF.Abs)
                    k.ts(dabT[gb][dr][:], dabT[gb][dr][:], 1.0, ALU.max)
                    k.recip(dabT[gb][dr][:], dabT[gb][dr][:])
                    k.tt(hhT[gb][dr][:], totT[gb][dr][:, :, 0:128],
                         V(dabT[gb][dr].h[:].unsqueeze(2).broadcast_to([128, 4, 128]), dabT[gb][dr].r), ALU.mult)
                    rows = slice(lo[dr] * 128, (lo[dr] + 4) * 128)
                    k.dma(hD[dr].sub(rows).re("(j p) e -> p j e", p=128)[:], hhT[gb][dr][:], q="gq")

            NG = NCH // 4
            phase1(0)
            for g in range(NG):
                if g + 1 < NG:
                    phase1(g + 1)
                phase2(g)
            for g in range(NCH // GC):
                b = g % 2
                rows = slice(g * GC * 128, (g + 1) * GC * 128)
                k.dma(hfl[b][:], hD[0].sub(rows).re("(g p) e -> p g e", p=128)[:])
                k.dma(hbl[b][:], hD[1].sub(rows).re("(g p) e -> p g e", p=128)[:])
                k.dma(obT[b][:], osrc.sub(rows).re("(g p) e -> p g e", p=128)[:])
                k.tt(hfl[b][:], hfl[b][:], hbl[b][:], ALU.add)
                k.tt(hfl[b][:], hfl[b][:], obT[b][:], ALU.mult)
                k.rsum(s1T[b][:], hfl[b][:])
                k.ts(s1T[b][:], s1T[b][:], 1.0 / 128, ALU.mult)
                k.tt(hfl[b][:], hfl[b][:], V(s1T[b].h[:].unsqueeze(2).broadcast_to([128, GC, 128]), s1T[b].r), ALU.subtract)
                k.act(sqtT[b][:], hfl[b][:], AF.Square)
                k.rsum(s2T[b][:], sqtT[b][:])
                k.act(s2T[b][:], s2T[b][:], AF.Sqrt, bias=c["epsb"][:, 0:1], scale=1.0 / 128)
                k.recip(s2T[b][:], s2T[b][:])
                k.tt(hfl[b][:], hfl[b][:], V(s2T[b].h[:].unsqueeze(2).broadcast_to([128, GC, 128]), s2T[b].r), ALU.mult)
                k.tt(ybT[b][:], hfl[b][:], V(gain.h[:, h * 128:(h + 1) * 128].unsqueeze(1).broadcast_to([128, GC, 128]), gain.r),
                     ALU.mult)
                pT = k.ps()
                for j in range(GC):
                    k.mm(pT[:, j * 128:(j + 1) * 128], ybT[b][:, j, :], c["identb"][:])
                grp = g // 4
                yc = ycs[grp % 2]
                k.copy(yc[:, (g % 4) * 512:(g % 4 + 1) * 512], pT[:], eng="act")
                if g % 4 == 3:
                    k.dma(c["brT"][1024 + h * 128:1024 + (h + 1) * 128, sb + grp * 2048:sb + (grp + 1) * 2048],
                          yc[:], q="gq")


def stage3_out(k, layer, T, c):
    TS = 256
    with Stage(k) as S:
        wbr = S.sb([128, 16, DM], BF16)
        wo = S.sb([128, 8, DM], BF16)
        gpost = S.sb([128, 8], F32)
        k.dma(wbr[:], c["w_br"].sub(layer).re("(a p) d -> p a d", p=128)[:], q="gq")
        k.dma(wo[:], c["w_out"].sub(layer).re("(a p) d -> p a d", p=128)[:], q="gq")
        k.dma(gpost[:], c["n_post"].sub(layer).re("(kc p) -> p kc", p=128)[:], slow=True)
        brs = [S.sb([128, 16, TS], BF16) for _ in range(2)]
        gps = [S.sb([128, 16, TS], BF16) for _ in range(2)]
        mgs = [S.sb([128, 32, TS], BF16) for _ in range(2)]
        xs = [S.sb([128, 8, TS], F32) for _ in range(2)]
        gated = S.sb([128, 16, TS], BF16)
        mergeds = [S.sb([128, 8, TS], F32) for _ in range(2)]
        mbfs = [S.sb([128, 8, TS], BF16) for _ in range(2)]
        ot = S.sb([128, 8, TS], F32)
        sq = S.sb([128, 8, TS], BF16)
        tmps = [S.sb([128, 2, TS], F32) for _ in range(4)]
        rstd = S.sb([128, TS], F32)
        tic = [0]

        def partA(t):
            sl = slice(t * TS, (t + 1) * TS)
            br = brs[t % 2]; gp = gps[t % 2]; mg = mgs[t % 2]; x = xs[t % 2]
            merged = mergeds[t % 2]
            k.dma(br[:], fm(c["brT"])[:, :, sl])
            k.dma(gp[:], fm(c["gpT"])[:, :, sl])
            k.dma(mg[:], fm(c["mgT"])[:, :, sl])
            k.dma(x[:], fm(c["xT"])[:, :, sl])
            k.tt(gated[:], br[:], gp[:], ALU.mult)
            for g in range(4):
                for dbp in range(4):
                    pb = k.ps()
                    for hf in range(2):
                        db = 2 * dbp + hf
                        for cc in range(4):
                            k.mm(pb[:, hf * TS:(hf + 1) * TS], wbr[:, g * 4 + cc, db * 128:(db + 1) * 128], gated[:, g * 4 + cc, :],
                                 start=(cc == 0), stop=(cc == 3))
                    pv = pb.re("p (a n) -> p a n", a=2)
                    db0 = 2 * dbp
                    if g == 0:
                        k.tt(merged[:, db0:db0 + 2, :], pv[:], mg[:, g * 8 + db0:g * 8 + db0 + 2, :], ALU.mult)
                    else:
                        tmp = tmps[tic[0] % 4]; tic[0] += 1
                        k.tt(tmp[:], pv[:], mg[:, g * 8 + db0:g * 8 + db0 + 2, :], ALU.mult)
                        k.tt(merged[:, db0:db0 + 2, :], merged[:, db0:db0 + 2, :], tmp[:], ALU.add)
            k.copy(mbfs[t % 2][:], merged[:], eng="act")

        def partB(t):
            sl = slice(t * TS, (t + 1) * TS)
            x = xs[t % 2]; mbf = mbfs[t % 2]
            for d2p in range(4):
                pb = k.ps()
                for hf in range(2):
                    d2 = 2 * d2p + hf
                    for kc in range(8):
                        k.mm(pb[:, hf * TS:(hf + 1) * TS], wo[:, kc, d2 * 128:(d2 + 1) * 128], mbf[:, kc, :],
                             start=(kc == 0), stop=(kc == 7))
                k.copy(ot[:, 2 * d2p:2 * d2p + 2, :], pb.re("p (a n) -> p a n", a=2)[:], eng="act")
            k.act(sq[:], ot[:], AF.Square)
            pb = k.ps()
            for kc in range(8):
                k.mm(pb[:, 0:TS], c["onesb"][:], sq[:, kc, :], start=(kc == 0), stop=(kc == 7))
            k.act(rstd[:], pb[:, 0:TS], AF.Sqrt, bias=c["epsb"][:, 0:1], scale=1.0 / DM)
            k.recip(rstd[:], rstd[:])
            for kc in range(8):
                k.stt(ot[:, kc, :], ot[:, kc, :], gpost[:, kc:kc + 1], rstd[:], ALU.mult, ALU.mult)
            k.tt(x[:], x[:], ot[:], ALU.add)
            k.dma(fm(c["xT"])[:, :, sl], x[:], q="gq")

        n_t = T // TS
        partA(0)
        for t in range(n_t):
            if t + 1 < n_t:
                partA(t + 1)
            partB(t)


_CACHE = {}


def run_cores(seqs_per_core, mems_per_core, weights, seq_lens, depth, debug=False, run_layers=None):
    key = (tuple(seq_lens), depth, debug, run_layers)
    if key not in _CACHE:
        _CACHE[key] = build(list(seq_lens), depth, debug, run_layers)
    nc, cst = _CACHE[key]
    T = sum(seq_lens)
    idx = bias_idx()
    rb = np.asarray(weights["rel_bias"], np.float32)
    biasT = np.stack([rb[idx[g // 4], g] for g in range(12)]).astype(np.float32)
    shared = {
        "w_in": np.ascontiguousarray(weights["w_in"][:depth], np.float32),
        "w_mem_kv": np.ascontiguousarray(weights["w_mem_kv"][:depth], np.float32),
        "w_branch": np.ascontiguousarray(weights["w_branch"][:depth], np.float32).reshape(depth, 2048, DM),
        "w_out": np.ascontiguousarray(weights["w_out"][:depth], np.float32),
        "norm_pre": np.ascontiguousarray(weights["norm_pre"][:depth], np.float32),
        "norm_post": np.ascontiguousarray(weights["norm_post"][:depth], np.float32),
        "mem_norm": np.ascontiguousarray(weights["mem_norm"][:depth], np.float32),
        "conv_qk": np.ascontiguousarray(weights["conv_qk"][:depth], np.float32),
        "gate_bias": np.ascontiguousarray(weights["mlstm_gate_bias"][:depth], np.float32).reshape(depth, 16),
        "head_gain": np.ascontiguousarray(weights["mlstm_head_gain"][:depth], np.float32),
        "biasT": biasT,
    }
    for kk, v in cst.items():
        shared["c_" + kk] = v
    in_maps = []
    for ci in range(8):
        xin = np.zeros((T, DM), np.float32)
        memin = np.zeros((len(seq_lens) * 256, DM), np.float32)
        off = 0
        for si, L in enumerate(seq_lens):
            if seqs_per_core[ci][si] is not None:
                xin[off:off + L] = seqs_per_core[ci][si]
                memin[si * 256:(si + 1) * 256] = mems_per_core[ci][si]
            off += L
        m = dict(shared)
        m["xin"] = xin
        m["memin"] = memin
        in_maps.append(m)
    res = run_bass_kernel_spmd(nc, in_maps, core_ids=list(range(8)))
    return res.results


def kernel(x_prompt, x_sample, mem_prompt, mem_sample, rel_bias, norm_pre, w_in, conv_qk,
           mlstm_gate_bias, mlstm_head_gain, mem_norm, w_mem_kv, w_branch, w_out, norm_post):
    weights = dict(rel_bias=rel_bias, norm_pre=norm_pre, w_in=w_in, conv_qk=conv_qk,
                   mlstm_gate_bias=mlstm_gate_bias, mlstm_head_gain=mlstm_head_gain, mem_norm=mem_norm,
                   w_mem_kv=w_mem_kv, w_branch=w_branch, w_out=w_out, norm_post=norm_post)
    x_prompt = np.asarray(x_prompt); x_sample = np.asarray(x_sample)
    LP, LS = x_prompt.shape[1], x_sample.shape[1]
    seqs = [[None, None] for _ in range(8)]
    mems = [[None, None] for _ in range(8)]
    for b in range(x_prompt.shape[0]):
        seqs[b][0] = x_prompt[b]; mems[b][0] = np.asarray(mem_prompt[b])
    for s in range(x_sample.shape[0]):
        seqs[4 + s][1] = x_sample[s]; mems[4 + s][1] = np.asarray(mem_sample[s])
    res = run_cores(seqs, mems, weights, (LP, LS), int(np.asarray(w_in).shape[0]))
    yp = np.stack([res[b]["yout"][0:LP] for b in range(x_prompt.shape[0])]).astype(np.float32)
    ys = np.stack([res[4 + s]["yout"][LP:LP + LS] for s in range(x_sample.shape[0])]).astype(np.float32)
    return (yp, ys)
```

```python
import numpy as np
from contextlib import ExitStack
import concourse.bass as bass
import concourse.mybir as mybir
from concourse.bass_utils import run_bass_kernel_spmd

F32 = mybir.dt.float32
BF16 = mybir.dt.bfloat16
AF = mybir.ActivationFunctionType
ALU = mybir.AluOpType
AX = mybir.AxisListType

DM = 1024
NIN = 13840
NEGM = -30000.0
EPS = 1e-6
O_QA, O_KA, O_VA, O_XB, O_QC, O_KC, O_VC, O_OC, O_GC, O_QD, O_GP, O_MG = (
    0, 1536, 3072, 4608, 5120, 5632, 6144, 6656, 7168, 7184, 7696, 9744)
DIL = (1, 4, 16)


class Reg:
    __slots__ = ("w", "rs")

    def __init__(self):
        self.w = None
        self.rs = []


class V:
    __slots__ = ("ap", "r")

    def __init__(self, ap, r):
        self.ap = ap
        self.r = r


class Buf:
    def __init__(self, h, r=None):
        self.h = h
        self.r = r if r is not None else Reg()

    def __getitem__(self, k):
        return V(self.h[k], self.r)

    def re(self, pat, **kw):
        return Buf(self.h.rearrange(pat, **kw), self.r)

    def sub(self, k):
        return Buf(self.h[k], self.r)


class Prog:
    NSLOT = 8
    BASE = {"pe": ("tensor", 1), "act": ("scalar", 1), "dve": ("vector", 1), "pool": ("gpsimd", 1)}
    DMAQ = {"sp": "sync", "gq": "gpsimd"}

    def __init__(self, nc, st):
        self.nc = nc
        self.STREAMS = dict(self.BASE)
        for q, phys in self.DMAQ.items():
            for i in range(self.NSLOT):
                self.STREAMS["%s%d" % (q, i)] = (phys, 16)
        self.phys = {p: [] for p in ("tensor", "scalar", "vector", "gpsimd", "sync")}
        self.cnt = {s: 0 for s in self.STREAMS}
        self.qcnt = {q: 0 for q in self.DMAQ}
        self.waited = {p: {s: 0 for s in self.STREAMS} for p in self.phys}
        self.sems = {s: st.enter_context(nc.semaphore("sem_" + s)) for s in self.STREAMS}
        self.nops = 0

    def op(self, stream, fn, reads=(), writes=()):
        deps = {}
        if stream in self.DMAQ:
            q = stream
            stream = "%s%d" % (q, self.qcnt[q] % self.NSLOT)
            self.qcnt[q] += 1
            if self.cnt[stream] > 0:
                deps[stream] = self.cnt[stream]
        phys = self.STREAMS[stream][0]

        def need(t):
            if t is not None and t[1] > deps.get(t[0], 0):
                deps[t[0]] = t[1]
        for r in reads:
            need(r.w)
        for r in writes:
            need(r.w)
            for t in r.rs:
                need(t)
        waits = []
        for s_, c_ in deps.items():
            if s_ == "pe" and stream == "pe":
                continue
            if c_ > self.waited[phys][s_]:
                self.waited[phys][s_] = c_
                waits.append((s_, c_))
        self.cnt[stream] += 1
        me = (stream, self.cnt[stream])
        for r in reads:
            r.rs.append(me)
            if len(r.rs) > 64:
                best = {}
                for t in r.rs:
                    if t[1] > best.get(t[0], 0):
                        best[t[0]] = t[1]
                r.rs = list(best.items())
        for r in writes:
            r.w = me
            r.rs = []
        self.phys[phys].append((stream, waits, fn))
        self.nops += 1

    def flush(self):
        final = [(s_, c_) for s_, c_ in self.cnt.items() if c_ > 0]
        for p in self.phys:
            w = [(s_, c_) for s_, c_ in final if c_ > self.waited[p][s_]]
            for s_, c_ in w:
                self.waited[p][s_] = c_
            self.phys[p].append((None, w, None))
        with self.nc.Block() as block:
            for phys, lst in self.phys.items():
                def body(engine, lst=lst):
                    for stream, waits, fn in lst:
                        for s_, c_ in waits:
                            engine.wait_ge(self.sems[s_], c_ * self.STREAMS[s_][1])
                        if fn is not None:
                            fn(engine).then_inc(self.sems[stream], self.STREAMS[stream][1])
                getattr(block, phys)(body)
        self.phys = {p: [] for p in self.phys}


class K:
    def __init__(self, nc, st):
        self.nc = nc
        self.P = Prog(nc, st)
        self.banks = [Buf(st.enter_context(nc.psum_tensor("psb%d" % i, [128, 512], F32))) for i in range(8)]
        self.bi = 0
        self.dq = 0

    def ps(self):
        b = self.banks[self.bi % 8]
        self.bi += 1
        return b

    def mm(self, out, lhsT, rhs, start=True, stop=True):
        self.P.op("pe", lambda e: e.matmul(out.ap, lhsT.ap, rhs.ap, start=start, stop=stop),
                  [lhsT.r, rhs.r], [out.r])

    def act(self, out, in_, func, bias=None, scale=None):
        kw = {}
        rd = [in_.r]
        if bias is not None:
            if isinstance(bias, V):
                kw["bias"] = bias.ap
                rd.append(bias.r)
            else:
                kw["bias"] = bias
        if scale is not None:
            if isinstance(scale, V):
                kw["scale"] = scale.ap
                rd.append(scale.r)
            else:
                kw["scale"] = scale
        self.P.op("act", lambda e: e.activation(out.ap, in_.ap, func, **kw), rd, [out.r])

    def _sc(self, s, rd):
        if isinstance(s, V):
            rd.append(s.r)
            return s.ap
        return s

    def tt(self, out, a, b, op, eng="dve"):
        self.P.op(eng, lambda e: e.tensor_tensor(out.ap, a.ap, b.ap, op), [a.r, b.r], [out.r])

    def ts(self, out, a, s1, op0, s2=None, op1=None, eng="dve"):
        rd = [a.r]
        s1 = self._sc(s1, rd)
        s2 = self._sc(s2, rd)
        if op1 is None:
            self.P.op(eng, lambda e: e.tensor_scalar(out.ap, a.ap, s1, None, op0), rd, [out.r])
        else:
            self.P.op(eng, lambda e: e.tensor_scalar(out.ap, a.ap, s1, s2, op0, op1), rd, [out.r])

    def stt(self, out, a, s, b, op0, op1, eng="dve"):
        rd = [a.r, b.r]
        s = self._sc(s, rd)
        self.P.op(eng, lambda e: e.scalar_tensor_tensor(out.ap, a.ap, s, b.ap, op0, op1), rd, [out.r])

    def copy(self, out, in_, eng="dve"):
        if eng == "act":
            self.P.op("act", lambda e: e.copy(out.ap, in_.ap), [in_.r], [out.r])
        else:
            self.P.op(eng, lambda e: e.tensor_copy(out.ap, in_.ap), [in_.r], [out.r])

    def memset(self, out, val, eng="dve"):
        self.P.op(eng, lambda e: e.memset(out.ap, val), [], [out.r])

    def recip(self, out, in_):
        self.P.op("dve", lambda e: e.reciprocal(out.ap, in_.ap), [in_.r], [out.r])

    def rsum(self, out, in_):
        self.P.op("dve", lambda e: e.reduce_sum(out.ap, in_.ap, AX.X), [in_.r], [out.r])

    def dma(self, out, in_, q="sp", slow=False):
        if slow:
            self.P.op(q, lambda e: e.dma_start(out=out.ap, in_=in_.ap, allow_slow_non_contiguous=True), [in_.r], [out.r])
        else:
            self.P.op(q, lambda e: e.dma_start(out=out.ap, in_=in_.ap), [in_.r], [out.r])


class Stage:
    def __init__(self, k):
        self.k = k

    def __enter__(self):
        self.st = ExitStack()
        self.n = 0
        return self

    def sb(self, shape, dt, name=None):
        self.n += 1
        K.uid = getattr(K, "uid", 0) + 1
        import os
        if os.environ.get("KDEBUG_SB"):
            print("sb", K.uid, shape, dt, "remaining", self.k.nc.sbuf_bytes_remaining)
        return Buf(self.st.enter_context(self.k.nc.sbuf_tensor("%s_%d" % (name or "t", K.uid), shape, dt)))

    def __exit__(self, *a):
        self.k.P.flush()
        self.st.close()
        return False


def t5_bucket(rel):
    nb = 16
    max_exact = 8
    ret = (rel > 0).astype(np.int32) * nb
    n = np.abs(rel)
    large = max_exact + (np.log(np.maximum(n, 1) / max_exact) / np.log(1024 / max_exact)
                         * (nb - max_exact)).astype(np.int32)
    large = np.minimum(large, nb - 1)
    return (ret + np.where(n < max_exact, n, large)).astype(np.int32)


def make_consts(seq_lens):
    c = {}
    c["ident"] = np.eye(128, dtype=np.float32)
    j = np.arange(128)
    ang = 2 * np.pi * np.outer(j, j) / 128.0
    C, S = np.cos(ang), np.sin(ang)
    c["cs_a"] = np.concatenate([C, -S], 1).astype(np.float32)
    c["cs_b"] = np.concatenate([S, C], 1).astype(np.float32)
    for L in sorted(set(seq_lens)):
        n2 = L // 128
        l2 = np.arange(n2)
        a = 2 * np.pi * np.outer(l2, j) / L
        tw = np.zeros((128, 4, 128), np.float32)
        tw[:n2, 0] = np.cos(a); tw[:n2, 1] = -np.sin(a)
        tw[:n2, 2] = np.sin(a); tw[:n2, 3] = np.cos(a)
        c["tw%d" % L] = tw
        a3 = 2 * np.pi * np.outer(l2, l2) / n2
        sc = 1.0 / np.sqrt(L * 128.0)
        m3 = np.zeros((128, 2, 128), np.float32)
        m3[:n2, 0, :n2] = np.cos(a3) * sc
        m3[:n2, 1, :n2] = np.sin(a3) * sc
        c["m3_%d" % L] = m3
    p = np.arange(128)[:, None]
    i = np.arange(256)[None, :]
    band = (p <= i) & (i <= p + 128)
    c["bandmask"] = np.where(band, 0.0, NEGM).astype(np.float32)
    u = np.arange(128)[:, None]
    t = np.arange(128)[None, :]
    tri = np.zeros((128, 5, 128), np.float32)
    tri[:, 0] = (u <= t); tri[:, 1] = (u > t); tri[:, 2] = 1.0; tri[:, 3] = (u >= t); tri[:, 4] = (u < t)
    c["tri"] = tri
    return c


def bias_idx():
    p = np.arange(128)[:, None]
    i = np.arange(256)[None, :]
    rel = np.clip(p + 64 - i, -64, 64)
    return np.stack([t5_bucket(d * rel) for d in DIL])


def build(seq_lens, depth, debug=False, run_layers=None):
    nc = bass.Bass("TRN2", target_bir_lowering=False)
    T = sum(seq_lens)
    NS = len(seq_lens)
    sbase = [sum(seq_lens[:i]) for i in range(NS)]
    NT = T // 512
    dt_in = lambda name, shape: Buf(nc.dram_tensor(name, shape, F32, kind="ExternalInput").ap())
    xin = dt_in("xin", [T, DM])
    memin = dt_in("memin", [NS * 256, DM])
    w_in = dt_in("w_in", [depth, DM, NIN])
    w_kv = dt_in("w_mem_kv", [depth, DM, DM])
    w_br = dt_in("w_branch", [depth, 4 * 512, DM])
    w_out = dt_in("w_out", [depth, DM, DM])
    n_pre = dt_in("norm_pre", [depth, DM])
    n_post = dt_in("norm_post", [depth, DM])
    n_mem = dt_in("mem_norm", [depth, DM])
    conv = dt_in("conv_qk", [depth, 5, DM])
    gbias = dt_in("gate_bias", [depth, 16])
    hgain = dt_in("head_gain", [depth, 512])
    biasT = dt_in("biasT", [12, 128, 256])
    cst = make_consts(seq_lens)
    cin = {k: dt_in("c_" + k, list(v.shape)) for k, v in cst.items()}
    yout = Buf(nc.dram_tensor("yout", [T, DM], F32, kind="ExternalOutput").ap())
    dram = lambda name, shape, dt: Buf(nc.dram_tensor(name, shape, dt).ap())
    xT = dram("xT", [DM, T], F32)
    hT = dram("hT", [DM, T], BF16)
    qkT = dram("qkT", [4608, T], BF16)
    tokm = dram("tokm", [T, 3584], BF16)
    gts = dram("gts", [T, 16], F32)
    gpT = dram("gpT", [2048, T], BF16)
    mgT = dram("mgT", [4096, T], BF16)
    brT = dram("brT", [2048, T], BF16)
    hF = dram("hF", [T, 128], F32)
    hB = dram("hB", [T, 128], F32)
    dbg = {}
    if debug:
        dbg["brT"] = Buf(nc.dram_tensor("dbg_brT", [2048, T], BF16, kind="ExternalOutput").ap())

    with ExitStack() as st:
        k = K(nc, st)
        G = Stage(k)
        G.__enter__()
        ident = G.sb([128, 128], F32)
        identb = G.sb([128, 128], BF16)
        onesb = G.sb([128, 128], BF16)
        k.dma(ident[:], cin["ident"][:])
        k.copy(identb[:], ident[:])
        k.memset(onesb[:], 1.0)
        epsb = G.sb([128, 1], F32)
        k.memset(epsb[:], EPS)
        csa = G.sb([128, 256], F32); csb = G.sb([128, 256], F32)
        csab = G.sb([128, 256], BF16); csbb = G.sb([128, 256], BF16)
        k.dma(csa[:], cin["cs_a"][:]); k.dma(csb[:], cin["cs_b"][:])
        k.copy(csab[:], csa[:]); k.copy(csbb[:], csb[:])
        tri = G.sb([128, 5, 128], F32)
        k.dma(tri[:], cin["tri"][:])
        bm = G.sb([128, 12, 256], F32)
        bmask = G.sb([128, 256], F32)
        k.dma(bmask[:], cin["bandmask"][:])
        k.dma(bm[:], biasT.re("g p i -> p g i")[:])
        for g in range(12):
            k.tt(bm[:, g, :], bm[:, g, :], bmask[:], ALU.add)
        k.P.flush()

        with Stage(k) as S:
            xi = [S.sb([128, DM], F32) for _ in range(2)]
            xo = [S.sb([128, 8, 128], F32) for _ in range(2)]
            for tb in range(T // 128):
                a = xi[tb % 2]; o = xo[tb % 2]
                k.dma(a[:], xin[tb * 128:(tb + 1) * 128, :])
                for half in range(2):
                    pb = k.ps()
                    for c4 in range(4):
                        kc = half * 4 + c4
                        k.mm(pb[:, c4 * 128:(c4 + 1) * 128], a[:, kc * 128:(kc + 1) * 128], ident[:])
                    k.copy(o[:, half * 4:half * 4 + 4, :], pb.re("p (a b) -> p a b", a=4)[:], eng="dve" if half else "act")
                k.dma(xT.re("(kc p) t -> p kc t", p=128)[:, :, tb * 128:(tb + 1) * 128], o[:], q="gq")

        for layer in range(depth if run_layers is None else run_layers):
            build_layer(k, layer, depth, seq_lens, sbase, T, dict(locals()))

        with Stage(k) as S:
            xi = [S.sb([128, 8, 128], F32) for _ in range(2)]
            xo = [S.sb([128, DM], F32) for _ in range(2)]
            for tb in range(T // 128):
                a = xi[tb % 2]; o = xo[tb % 2]
                k.dma(a[:], xT.re("(kc p) t -> p kc t", p=128)[:, :, tb * 128:(tb + 1) * 128])
                for half in range(2):
                    pb = k.ps()
                    for c4 in range(4):
                        kc = half * 4 + c4
                        k.mm(pb[:, c4 * 128:(c4 + 1) * 128], a[:, kc, :], ident[:])
                    k.copy(o[:, half * 512:(half + 1) * 512], pb[:], eng="dve" if half else "act")
                k.dma(yout[tb * 128:(tb + 1) * 128, :], o[:], q="gq")
            if debug:
                k.dma(dbg["brT"][:], brT[:], q="gq")
        G.k.P.flush()
        G.st.close()
        print('total ops', k.P.nops, k.P.cnt)
    return nc, cst


def build_layer(k, layer, depth, seq_lens, sbase, T, c):
    import os
    en = os.environ.get("KSTAGES", "01DBAC3")
    if "0" in en:
        stage0_norm(k, layer, T, c)
    if "1" in en:
        stage1_proj(k, layer, T, c)
    for si, L in enumerate(seq_lens):
        if "D" in en:
            stageD(k, layer, si, sbase[si], L, c)
        if "B" in en:
            stageB(k, layer, si, sbase[si], L, c)
        if "A" in en:
            stageA(k, layer, si, sbase[si], L, c)
        if "C" in en:
            stageC(k, layer, si, sbase[si], L, c)
    if "3" in en:
        stage3_out(k, layer, T, c)


def zero_rows(k, c, r0, sb, L):
    with Stage(k) as S:
        z = S.sb([128, 4, 1024], BF16)
        k.memset(z[:], 0.0)
        for t in range(L // 1024):
            k.dma(fm(c["brT"].sub(slice(r0, r0 + 512)))[:, :, sb + t * 1024: sb + (t + 1) * 1024], z[:], q="gq")


def ssl(start, count, step):
    return slice(start, start + (count - 1) * step + 1, step)


def fm(buf):
    return buf.re("(kc p) t -> p kc t", p=128)


def stage0_norm(k, layer, T, c):
    with Stage(k) as S:
        gpre = S.sb([128, 8], F32)
        k.dma(gpre[:], c["n_pre"].sub(layer).re("(kc p) -> p kc", p=128)[:], slow=True)
        xs = [S.sb([128, 8, 512], F32) for _ in range(2)]
        sq = S.sb([128, 8, 512], BF16)
        hs = [S.sb([128, 8, 512], BF16) for _ in range(2)]
        rstd = S.sb([128, 512], F32)
        for t in range(T // 512):
            x = xs[t % 2]; h = hs[t % 2]
            sl = slice(t * 512, (t + 1) * 512)
            k.dma(x[:], fm(c["xT"])[:, :, sl])
            k.act(sq[:], x[:], AF.Square)
            pb = k.ps()
            for kc in range(8):
                k.mm(pb[:], c["onesb"][:], sq[:, kc, :], start=(kc == 0), stop=(kc == 7))
            k.act(rstd[:], pb[:], AF.Sqrt, bias=c["epsb"][:, 0:1], scale=1.0 / DM)
            k.recip(rstd[:], rstd[:])
            for kc in range(8):
                k.stt(h[:, kc, :], x[:, kc, :], gpre[:, kc:kc + 1], rstd[:], ALU.mult, ALU.mult)
            k.dma(fm(c["hT"])[:, :, sl], h[:], q="gq")


def stage1_proj(k, layer, T, c):
    w_in = c["w_in"].sub(layer).re("(kc p) n -> p kc n", p=128)
    NT = T // 512
    with Stage(k) as S:
        wA = S.sb([128, 8, 3600], BF16)
        k.dma(wA[:, :, 0:1536], w_in[:, :, O_VA:O_VA + 1536], q="gq")
        k.dma(wA[:, :, 2560:3584], w_in[:, :, O_VC:O_VC + 1024], q="gq")
        k.dma(wA[:, :, 3584:3600], w_in[:, :, O_GC:O_GC + 16], q="gq")
        wxb = S.sb([128, 8, 512], F32)
        wxT = S.sb([128, 4, 1024], F32)
        k.dma(wxb[:], w_in[:, :, O_XB:O_XB + 512])
        for g in range(4):
            for half in range(2):
                pb = k.ps()
                for c4 in range(4):
                    kc = half * 4 + c4
                    k.mm(pb[:, c4 * 128:(c4 + 1) * 128], wxb[:, kc, g * 128:(g + 1) * 128], c["ident"][:])
                k.copy(wxT[:, g, half * 512:(half + 1) * 512], pb[:])
        for g in range(4):
            for kc in range(8):
                pb = k.ps()
                k.mm(pb[:, 0:256], wxT[:, g, kc * 128:(kc + 1) * 128], c["csa"][:])
                k.copy(wA[:, kc, 1536 + g * 256:1536 + (g + 1) * 256], pb[:, 0:256], eng="act" if kc % 2 else "dve")
        hs = [S.sb([128, 8, 512], BF16) for _ in range(2)]
        os_ = [S.sb([128, 4, 3584], BF16) for _ in range(2)]
        og = [S.sb([128, 4, 16], F32) for _ in range(2)]
        ev = 0
        for t in range(NT):
            h = hs[t % 2]; o = os_[t % 2]; g_ = og[t % 2]
            sl = slice(t * 512, (t + 1) * 512)
            k.dma(h[:], fm(c["hT"])[:, :, sl])
            for tb in range(4):
                for cc in range(8):
                    c0 = cc * 512
                    w = 512 if cc < 7 else 16
                    pb = k.ps()
                    for kc in range(8):
                        k.mm(pb[:, 0:w], h[:, kc, tb * 128:(tb + 1) * 128], wA[:, kc, c0:c0 + w],
                             start=(kc == 0), stop=(kc == 7))
                    if cc == 7:
                        k.copy(g_[:, tb, :], pb[:, 0:16])
                    elif cc == 6:
                        k.act(o[:, tb, c0:c0 + 512], pb[:], AF.Sigmoid)
                    else:
                        k.copy(o[:, tb, c0:c0 + 512], pb[:], eng="act" if ev % 2 else "dve")
                        ev += 1
            k.dma(c["tokm"].sub(sl).re("(tb p) n -> p tb n", p=128)[:], o[:], q="gq")
            k.dma(c["gts"].sub(sl).re("(tb p) n -> p tb n", p=128)[:], g_[:], q="gq")

    def fm_pass(specs):
        ncols = sum(s[1] for s in specs)
        nb = ncols // 128
        with Stage(k) as S:
            w = S.sb([128, 8, ncols], BF16)
            blocks = []
            o0 = 0
            for (wc, n, dest, r0, func, scale) in specs:
                k.dma(w[:, :, o0:o0 + n], w_in[:, :, wc:wc + n], q="gq")
                for b in range(n // 128):
                    blocks.append((o0 + b * 128, func, scale))
                o0 += n
            hs = [S.sb([128, 8, 512], BF16) for _ in range(2)]
            os_ = [S.sb([128, nb, 512], BF16) for _ in range(2)]
            ev = 0
            for t in range(NT):
                h = hs[t % 2]; o = os_[t % 2]
                sl = slice(t * 512, (t + 1) * 512)
                k.dma(h[:], fm(c["hT"])[:, :, sl])
                for bi, (wc, func, scale) in enumerate(blocks):
                    pb = k.ps()
                    for kc in range(8):
                        k.mm(pb[:], w[:, kc, wc:wc + 128], h[:, kc, :], start=(kc == 0), stop=(kc == 7))
                    if func is not None:
                        k.act(o[:, bi, :], pb[:], func)
                    elif ev % 2:
                        k.act(o[:, bi, :], pb[:], AF.Copy, scale=scale)
                        ev += 1
                    else:
                        k.ts(o[:, bi, :], pb[:], scale, ALU.mult)
                        ev += 1
                o0 = 0
                for (wc, n, dest, r0, func, scale) in specs:
                    k.dma(fm(dest.sub(slice(r0, r0 + n)))[:, :, sl], o[:, o0 // 128:(o0 + n) // 128, :], q="gq")
                    o0 += n
    sq = 128.0 ** -0.5
    qk = c["qkT"]
    fm_pass([(O_QA, 1536, qk, 0, None, sq), (O_KA, 1536, qk, 1536, None, 1.0), (O_QC, 1024, qk, 3072, None, 1.0),
             (O_QD, 512, qk, 4096, None, sq)])
    fm_pass([(O_GP, 2048, c["gpT"], 0, AF.Silu, 1.0), (O_MG, 1024, c["mgT"], 0, AF.Sigmoid, 1.0)])
    fm_pass([(O_MG + 1024, 3072, c["mgT"], 1024, AF.Sigmoid, 1.0)])


def stageD(k, layer, si, sb, L, c):
    with Stage(k) as S:
        wkv = S.sb([128, 8, 1024], BF16)
        k.dma(wkv[:], c["w_kv"].sub(layer).re("(kc p) n -> p kc n", p=128)[:], q="gq")
        gm = S.sb([128, DM], F32)
        k.dma(gm[:], V(c["n_mem"].h[layer:layer + 1, :].broadcast_to([128, DM]), c["n_mem"].r))
        mhT = S.sb([128, 8, 256], BF16)
        for mb in range(2):
            m = S.sb([128, DM], F32)
            sqt = S.sb([128, DM], F32)
            ss = S.sb([128, 1], F32)
            k.dma(m[:], c["memin"][si * 256 + mb * 128: si * 256 + (mb + 1) * 128, :])
            k.act(sqt[:], m[:], AF.Square)
            k.rsum(ss[:], sqt[:])
            k.act(ss[:], ss[:], AF.Sqrt, bias=c["epsb"][:, 0:1], scale=1.0 / DM)
            k.recip(ss[:], ss[:])
            k.stt(m[:], m[:], ss[:, 0:1], gm[:], ALU.mult, ALU.mult)
            for half in range(2):
                pb = k.ps()
                for c4 in range(4):
                    kc = half * 4 + c4
                    k.mm(pb[:, c4 * 128:(c4 + 1) * 128], m[:, kc * 128:(kc + 1) * 128], c["ident"][:])
                k.copy(mhT[:, half * 4:half * 4 + 4, mb * 128:(mb + 1) * 128], pb.re("p (a b) -> p a b", a=4)[:])
        KT = S.sb([128, 4, 256], BF16)
        Vd = S.sb([128, 2, 512], BF16)
        for h in range(4):
            pb = k.ps()
            for kc in range(8):
                k.mm(pb[:, 0:256], wkv[:, kc, h * 128:(h + 1) * 128], mhT[:, kc, :], start=(kc == 0), stop=(kc == 7))
            k.copy(KT[:, h, :], pb[:, 0:256])
        for mb in range(2):
            pb = k.ps()
            for kc in range(8):
                k.mm(pb[:], mhT[:, kc, mb * 128:(mb + 1) * 128], wkv[:, kc, 512:1024], start=(kc == 0), stop=(kc == 7))
            k.copy(Vd[:, mb, :], pb[:])
        qs = [S.sb([128, 4, 512], BF16) for _ in range(2)]
        os_ = [S.sb([128, 4, 512], BF16) for _ in range(2)]
        Es = [S.sb([128, 2, 512], BF16) for _ in range(2)]
        rd = S.sb([128, 512], F32)
        qd = fm(c["qkT"].sub(slice(4096, 4608)))
        yd = fm(c["brT"].sub(slice(1536, 2048)))
        it = 0
        for t in range(L // 512):
            q = qs[t % 2]; o = os_[t % 2]
            sl = slice(sb + t * 512, sb + (t + 1) * 512)
            k.dma(q[:], qd[:, :, sl])
            for h in range(4):
                E = Es[it % 2]; it += 1
                for mb in range(2):
                    pb = k.ps()
                    k.mm(pb[:], KT[:, h, mb * 128:(mb + 1) * 128], q[:, h, :])
                    k.act(E[:, mb, :], pb[:], AF.Exp)
                po = k.ps(); pd = k.ps()
                for mb in range(2):
                    k.mm(po[:], Vd[:, mb, h * 128:(h + 1) * 128], E[:, mb, :], start=(mb == 0), stop=(mb == 1))
                for mb in range(2):
                    k.mm(pd[:], c["onesb"][:], E[:, mb, :], start=(mb == 0), stop=(mb == 1))
                k.recip(rd[:], pd[:])
                k.tt(o[:, h, :], po[:], rd[:], ALU.mult)
            k.dma(yd[:, :, sl], o[:], q="gq")


def stageA(k, layer, si, sb, L, c):
    QR = 4096
    qk = c["qkT"]; tokm = c["tokm"]; bm = c["bm"]
    with Stage(k) as S:
        qbs = [S.sb([128, QR], BF16) for _ in range(2)]
        kbs = [S.sb([128, QR + 4096], BF16) for _ in range(2)]
        vbs = [S.sb([128, (QR + 4096) // 128, 128], BF16) for _ in range(2)]
        nd = S.sb([128, 2, QR], F32)
        yas = [S.sb([128, QR], BF16) for _ in range(2)]
        PT2s = [S.sb([128, 512], BF16) for _ in range(6)]
        tmp2s = [S.sb([128, 512], F32) for _ in range(4)]
        pi2 = 0
        it = 0; pi = 0; ri = 0
        for h in range(4):
            for Q0 in range(0, L, QR):
                Q1 = Q0 + QR
                for gi, d in enumerate(DIL):
                    M = L // d; NB = M // 128
                    KP0 = max(0, Q0 - 128 * d); KP1 = min(L, Q1 + 128 * d)
                    j0 = KP0 // (128 * d); j1 = KP1 // (128 * d); NBL = j1 - j0
                    qb = qbs[it % 2]; kb = kbs[it % 2]; vb = vbs[it % 2]; it += 1
                    qrow = gi * 512 + h * 128; krow = 1536 + gi * 512 + h * 128
                    k.dma(qb[:, 0:QR], qk[qrow:qrow + 128, sb + Q0:sb + Q1])
                    k.dma(kb[:, 0:KP1 - KP0], qk[krow:krow + 128, sb + KP0:sb + KP1])
                    vcol = gi * 512 + h * 128
                    for r in range(d):
                        src = tokm.sub((slice(sb + KP0 + r, sb + KP1, d), slice(vcol, vcol + 128)))
                        k.dma(vb[:, r * NBL:(r + 1) * NBL, :], src.re("(j p) n -> p j n", p=128)[:])
                    g12 = gi * 4 + h
                    a0 = Q0 // d; a1 = Q1 // d
                    jlo = max(0, a0 // 128 - 1); jhi = min(NB - 1, a1 // 128)
                    pts = []; segs = []
                    for r in range(d):
                        for j in range(jlo, jhi + 1):
                            w0 = 128 * j - 64
                            pts.append((r, j, max(0, a0 - w0), min(256, a1 - w0)))
                        for n in range(jlo, jhi + 2):
                            s0 = max(128 * n - 64, a0); s1 = min(128 * n + 64, a1)
                            if s1 <= s0:
                                continue
                            parts = []
                            if jlo <= n - 1 <= jhi:
                                parts.append((n - 1, 192 - 128 * n))
                            if jlo <= n <= jhi:
                                parts.append((n, 64 - 128 * n))
                            segs.append((r, n, s0, s1, parts))
                    done = {}
                    si_ = 0
                    GW = 4
                    for w_ in range(0, len(pts), GW):
                        wave = pts[w_:w_ + GW]
                        pend = []
                        for b0 in range(0, len(wave), 2):
                            pair = wave[b0:b0 + 2]
                            cur = k.ps()
                            P2 = PT2s[pi2 % len(PT2s)]; tmp2 = tmp2s[pi2 % len(tmp2s)]; pi2 += 1
                            for ix, (r, j, c0, c1) in enumerate(pair):
                                ks = r + d * 128 * j - KP0
                                qs = r + d * (128 * j - 64 + c0) - Q0
                                k.mm(cur[:, ix * 256 + c0:ix * 256 + c1], kb[:, ssl(ks, 128, d)], qb[:, ssl(qs, c1 - c0, d)])
                            pend.append((cur, P2, tmp2, pair))
                        for cur, P2, tmp2, pair in pend:
                            if len(pair) == 2:
                                k.tt(tmp2.re("p (a n) -> p a n", a=2)[:], cur.re("p (a n) -> p a n", a=2)[:],
                                     V(bm.h[:, g12, :].unsqueeze(1).broadcast_to([128, 2, 256]), bm.r), ALU.add)
                            else:
                                r, j, c0, c1 = pair[0]
                                k.tt(tmp2[:, c0:c1], cur[:, c0:c1], bm[:, g12, c0:c1], ALU.add)
                        for cur, P2, tmp2, pair in pend:
                            if len(pair) == 2:
                                k.act(P2[:], tmp2[:], AF.Exp)
                            else:
                                r, j, c0, c1 = pair[0]
                                k.act(P2[:, c0:c1], tmp2[:, c0:c1], AF.Exp)
                            for ix, (r, j, c0, c1) in enumerate(pair):
                                done[(r, j)] = P2.sub((slice(None), slice(ix * 256, (ix + 1) * 256)))
                        pi += len(wave)
                        ready = []
                        while si_ < len(segs) and all((segs[si_][0], jj) in done for jj, _ in segs[si_][4]):
                            ready.append(segs[si_]); si_ += 1
                        pbs = []
                        for (r, n, s0, s1, parts) in ready:
                            w = s1 - s0
                            pb2 = k.ps(); pbs.append(pb2)
                            for col0, use_v in ((0, True), (128, False)):
                                for ix, (jj, off) in enumerate(parts):
                                    lhs = vb[:, r * NBL + (jj - j0), :] if use_v else c["onesb"][:]
                                    k.mm(pb2[:, col0:col0 + w], lhs, done[(r, jj)][:, s0 + off:s1 + off],
                                         start=(ix == 0), stop=(ix == len(parts) - 1))
                        for (r, n, s0, s1, parts), pb2 in zip(ready, pbs):
                            w = s1 - s0
                            ls = r + d * s0 - Q0
                            dst = nd[:, :, ssl(ls, w, d)]
                            srcv = pb2.re("p (a n) -> p a n", a=4)[:, 0:2, 0:w]
                            if gi == 0:
                                k.copy(dst, srcv, eng="act")
                            else:
                                k.tt(dst, dst, srcv, ALU.add)
                    assert si_ == len(segs)
                ya = yas[ri % 2]; ri += 1
                k.recip(nd[:, 1, :], nd[:, 1, :])
                k.tt(ya[:], nd[:, 0, :], nd[:, 1, :], ALU.mult)
                k.dma(c["brT"][h * 128:(h + 1) * 128, sb + Q0:sb + Q1], ya[:], q="gq")


def stageB(k, layer, si, sb, L, c):
    N2 = L // 128
    with Stage(k) as S:
        tw = S.sb([128, 4, 128], F32)
        m3 = S.sb([128, 2, 128], F32)
        m3b = S.sb([128, 2, 128], BF16)
        k.dma(tw[:], c["cin"]["tw%d" % L][:])
        k.dma(m3[:], c["cin"]["m3_%d" % L][:])
        k.copy(m3b[:], m3[:])
        nbuf = 2 if N2 <= 64 else 1
        Zss = [S.sb([128, N2, 256], BF16) for _ in range(nbuf)]
        Zss = Zss * (2 // nbuf)
        Ap = S.sb([128, 128, 256], BF16)
        Ybs = [S.sb([128, 128, 128], BF16) for _ in range(nbuf)]
        Ybs = Ybs * (2 // nbuf)
        tAs = [S.sb([128, 2, 2, 128], F32) for _ in range(2)]
        tBs = [S.sb([128, 2, 2, 128], F32) for _ in range(2)]
        for fg in range(4):
            Zs = Zss[fg % 2]; Yb = Ybs[fg % 2]
            src = c["tokm"].sub((slice(sb, sb + L), slice(1536 + fg * 256, 1536 + (fg + 1) * 256)))
            k.dma(Zs[:], src.re("(l1 l2) n -> l1 l2 n", l2=N2)[:])
            for cp in range(64):
                pb = k.ps()
                for cc in range(2):
                    ch = 2 * cp + cc
                    k.mm(pb[0:N2, cc * 256:(cc + 1) * 256], Zs[:, :, ch], c["csab"][:], start=True, stop=False)
                    k.mm(pb[0:N2, cc * 256:(cc + 1) * 256], Zs[:, :, 128 + ch], c["csbb"][:], start=False, stop=True)
                pv = pb.re("p (a b n) -> p a b n", a=2, b=2)
                tA = tAs[cp % 2]; tB = tBs[cp % 2]
                shp = [N2, 2, 2, 128]
                bc = lambda buf, ap: V(ap.broadcast_to(shp), buf.r)
                k.tt(tA[0:N2], bc(pb, pv.h[0:N2, :, 0, :].unsqueeze(2)), bc(tw, tw.h[0:N2, 0:2, :].unsqueeze(1)), ALU.mult)
                k.tt(tB[0:N2], bc(pb, pv.h[0:N2, :, 1, :].unsqueeze(2)), bc(tw, tw.h[0:N2, 2:4, :].unsqueeze(1)), ALU.mult)
                k.tt(Ap.re("p c (b n) -> p c b n", b=2)[0:N2, 2 * cp:2 * cp + 2, :, :], tA[0:N2], tB[0:N2], ALU.add)
            for c4 in range(32):
                pb = k.ps()
                pv = pb.re("p (a n) -> p a n", a=4)
                k.mm(pv[0:N2], m3b[0:N2, 0, 0:N2], Ap[0:N2, 4 * c4:4 * c4 + 4, 0:128], start=True, stop=False)
                k.mm(pv[0:N2], m3b[0:N2, 1, 0:N2], Ap[0:N2, 4 * c4:4 * c4 + 4, 128:256], start=False, stop=True)
                k.copy(Yb[0:N2, 4 * c4:4 * c4 + 4, :], pv[0:N2], eng="act" if c4 % 2 else "dve")
            dst = c["brT"].sub((slice(512 + fg * 128, 512 + (fg + 1) * 128), slice(sb, sb + L)))
            dv = dst.re("c (k2 k1) -> k2 c k1", k1=128)
            for cb in range(8):
                k.dma(dv[:, cb * 16:(cb + 1) * 16, :], Yb[0:N2, cb * 16:(cb + 1) * 16, :], q="gq")


def stageC(k, layer, si, sb, L, c):
    NCH = L // 128
    CW = 1024
    tri = c["tri"]; qk = c["qkT"]; tokm = c["tokm"]
    with Stage(k) as S:
        gb = S.sb([128, 16], F32)
        gain = S.sb([128, 512], F32)
        cw = S.sb([128, 8, 5], F32)
        one1 = S.sb([128, 1], F32)
        k.memset(one1[:], 1.0)
        k.dma(gb[:], V(c["gbias"].h[layer:layer + 1, :].broadcast_to([128, 16]), c["gbias"].r))
        k.dma(gain[:], V(c["hgain"].h[layer:layer + 1, :].broadcast_to([128, 512]), c["hgain"].r))
        for blk in range(8):
            k.dma(cw[:, blk, :], c["conv"].sub((layer, slice(0, 5), slice(blk * 128, (blk + 1) * 128))).re("j p -> p j")[:], slow=True)
        g_sb = S.sb([128, NCH, 16], F32)
        k.dma(g_sb[:], c["gts"].sub(slice(sb, sb + L)).re("(n p) g -> p n g", p=128)[:])
        k.tt(g_sb[:], g_sb[:], V(gb.h[:].unsqueeze(1).broadcast_to([128, NCH, 16]), gb.r), ALU.add)
        qT = S.sb([128, L], BF16); kT = S.sb([128, L], BF16)
        v_sb = S.sb([128, NCH, 129], BF16)
        lf = S.sb([128, 2, NCH], F32); e1 = S.sb([128, NCH], F32); tw_ = S.sb([128, NCH], F32)
        EB = S.sb([128, 2, NCH], F32); WA = S.sb([128, 2, NCH], F32); SG = S.sb([128, 2, NCH], F32)
        raws = [S.sb([128, CW + 4], BF16) for _ in range(2)]
        accs = [S.sb([128, 512], F32) for _ in range(2)]
        dgs = [S.sb([128, 5, 128], BF16) for _ in range(2)]
        ai = 0
        mk2 = lambda shape, dt: [[S.sb(shape, dt) for _ in range(2)] for _ in range(2)]
        Cst = [S.sb([128, 129], F32) for _ in range(2)]
        CbfS = [[S.sb([128, 129], BF16) for _ in range(8)] for _ in range(2)]
        def mk1(shape, dt):
            one = [S.sb(shape, dt) for _ in range(2)]
            return [one, one]
        SmT = mk2([128, 4, 128], F32); At = mk1([128, 4, 128], F32); Et = At
        sqkT = mk2([128, 4, 128], BF16); KwT = mk2([128, 4, 128], BF16)
        intraT = mk2([128, 4, 129], F32); clocT = mk2([128, 4, 129], F32); totT = mk1([128, 4, 129], F32)
        dabT = mk2([128, 4], F32); hhT = mk1([128, 4, 128], F32)
        GC = 4
        hfl = [S.sb([128, GC, 128], F32) for _ in range(2)]
        hbl = [S.sb([128, GC, 128], F32)] * 2
        obT = [S.sb([128, GC, 128], BF16)] * 2
        s1T = [S.sb([128, GC], F32) for _ in range(2)]
        s2T = [S.sb([128, GC], F32) for _ in range(2)]
        sqtT = [S.sb([128, GC, 128], F32)] * 2
        ybT = [S.sb([128, GC, 128], BF16) for _ in range(2)]
        ycs = [S.sb([128, 2048], BF16)] * 2
        ci = 0
        for h in range(4):
            for dr in range(2):
                fcol = g_sb[:, :, dr * 8 + 4 + h]
                k.act(e1[:], fcol, AF.Exp, scale=-1.0)
                k.act(lf[:, dr, :], e1[:], AF.Ln, bias=one1[:, 0:1], scale=1.0)
            for dr in range(2):
                icol = g_sb[:, :, dr * 8 + h]
                pb = k.ps()
                k.mm(pb[:, 0:NCH], tri[:, 0 if dr == 0 else 3, :], lf[:, dr, :])
                k.act(EB[:, dr, :], pb[:, 0:NCH], AF.Exp, scale=-1.0)
                pb = k.ps()
                k.mm(pb[:, 0:NCH], tri[:, 1 if dr == 0 else 4, :], lf[:, dr, :])
                k.stt(tw_[:], pb[:, 0:NCH], -1.0, icol, ALU.mult, ALU.add)
                k.act(WA[:, dr, :], tw_[:], AF.Exp)
                pb = k.ps()
                k.mm(pb[:, 0:NCH], tri[:, 2, :], lf[:, dr, :])
                k.act(SG[:, dr, :], pb[:, 0:NCH], AF.Exp, scale=-1.0)
            for which in range(2):
                row = 3072 + which * 512 + h * 128
                blk = which * 4 + h
                dg = dgs[which]
                for j in range(5):
                    k.ts(dg[:, j, :], c["identb"][:], cw[:, blk, j:j + 1], ALU.mult)
                for c0 in range(0, L, CW):
                    raw = raws[ci % 2]; ci += 1
                    lo = max(0, c0 - 2); hi = min(L, c0 + CW + 2)
                    if c0 == 0:
                        k.memset(raw[:, 0:2], 0.0)
                    if c0 + CW >= L:
                        k.memset(raw[:, CW + 2:CW + 4], 0.0)
                    k.dma(raw[:, lo - (c0 - 2):hi - (c0 - 2)], qk[row:row + 128, sb + lo:sb + hi])
                    for hf in range(CW // 512):
                        pb = k.ps()
                        for j in range(5):
                            k.mm(pb[:], dg[:, j, :], raw[:, hf * 512 + j:hf * 512 + j + 512], start=(j == 0), stop=(j == 4))
                        cols = slice(c0 + hf * 512, c0 + (hf + 1) * 512)
                        if which == 0:
                            acc = accs[ai % 2]; ai += 1
                            k.act(acc[:], pb[:], AF.Silu)
                            k.ts(qT[:, cols], acc[:], 128.0 ** -0.5, ALU.mult)
                        else:
                            k.act(kT[:, cols], pb[:], AF.Silu)
            vsrc = tokm.sub((slice(sb, sb + L), slice(2560 + h * 128, 2560 + (h + 1) * 128)))
            k.dma(v_sb[:, :, 0:128], vsrc.re("(n p) n2 -> p n n2", p=128)[:])
            k.memset(v_sb[:, :, 128:129], 1.0)
            osrc = tokm.sub((slice(sb, sb + L), slice(3072 + h * 128, 3072 + (h + 1) * 128)))
            hD = [c["hF"].sub(slice(sb, sb + L)), c["hB"].sub(slice(sb, sb + L))]
            for dr in range(2):
                k.memset(Cst[dr][:], 0.0)
                k.memset(CbfS[dr][7][:], 0.0)
            mS = (0, 3)
            mA = (1, 4)
            mR = (0, 3)
            r2 = lambda bank: bank.re("p (a n) -> p a n", a=2)
            r4 = lambda bank: bank.re("p (a n) -> p a n", a=4)
            def gctx(g):
                lo = (4 * g, NCH - 4 - 4 * g)
                chunk = lambda dr, j: lo[dr] + j
                tslf = lambda dr, j: slice(chunk(dr, j) * 128, (chunk(dr, j) + 1) * 128)
                return g % 2, lo, chunk, tslf

            def phase1(g):
                gb, lo, chunk, tslf = gctx(g)
                bS = [k.ps(), k.ps()]
                for dr in range(2):
                    for j in range(4):
                        k.mm(bS[dr][:, j * 128:(j + 1) * 128], kT[:, tslf(dr, j)], qT[:, tslf(dr, j)])
                bK = [k.ps(), k.ps()]
                for dr in range(2):
                    for j in range(4):
                        k.mm(bK[dr][:, j * 128:(j + 1) * 128], kT[:, tslf(dr, j)], c["identb"][:])
                b4 = lambda buf, ap: V(ap.unsqueeze(2).broadcast_to([128, 4, 128]), buf.r)
                m4 = lambda mi: V(tri.h[:, mi, :].unsqueeze(1).broadcast_to([128, 4, 128]), tri.r)
                for dr in range(2):
                    k.tt(At[gb][dr][:], m4(mA[dr]), b4(lf, lf.h[:, dr, lo[dr]:lo[dr] + 4]), ALU.mult)
                for dr in range(2):
                    k.tt(SmT[gb][dr][:], r4(bS[dr])[:], V(tri.h[:, mS[dr], :].unsqueeze(1).broadcast_to([128, 4, 128]), tri.r), ALU.mult)
                for dr in range(2):
                    k.tt(KwT[gb][dr][:], r4(bK[dr])[:], b4(WA, WA.h[:, dr, lo[dr]:lo[dr] + 4]), ALU.mult)
                bD = [k.ps(), k.ps()]
                for dr in range(2):
                    for j in range(4):
                        k.mm(bD[dr][:, j * 128:(j + 1) * 128], At[gb][dr][:, j, :], tri[:, mR[dr], :])
                for dr in range(2):
                    gcol = dr * 8 + h
                    k.stt(Et[gb][dr][:], r4(bD[dr])[:], -1.0, b4(g_sb, g_sb.h[:, lo[dr]:lo[dr] + 4, gcol]), ALU.mult, ALU.add)
                    k.act(Et[gb][dr][:], Et[gb][dr][:], AF.Exp)
                bC = [[k.ps(), k.ps()], [k.ps(), k.ps()]]
                for dr in range(2):
                    for j in range(4):
                        k.mm(bC[dr][j // 2][:, (j % 2) * 256:(j % 2) * 256 + 129], KwT[gb][dr][:, j, :], v_sb[:, chunk(dr, j), :])
                for dr in range(2):
                    k.tt(sqkT[gb][dr][:], Et[gb][dr][:], SmT[gb][dr][:], ALU.mult)
                for dr in range(2):
                    for hf in range(2):
                        k.copy(clocT[gb][dr][:, 2 * hf:2 * hf + 2, :], r2(bC[dr][hf])[:, :, 0:129])
                bI = [[k.ps(), k.ps()], [k.ps(), k.ps()]]
                for dr in range(2):
                    for j in range(4):
                        k.mm(bI[dr][j // 2][:, (j % 2) * 256:(j % 2) * 256 + 129], sqkT[gb][dr][:, j, :], v_sb[:, chunk(dr, j), :])
                for dr in range(2):
                    for hf in range(2):
                        k.copy(intraT[gb][dr][:, 2 * hf:2 * hf + 2, :], r2(bI[dr][hf])[:, :, 0:129], eng="act")

            def phase2(g):
                gb, lo, chunk, tslf = gctx(g)
                for pos in range(4):
                    for dr in range(2):
                        j = pos if dr == 0 else 3 - pos
                        n = chunk(dr, j)
                        st_ = 4 * g + pos
                        k.stt(Cst[dr][:], Cst[dr][:], SG[:, dr, n:n + 1], clocT[gb][dr][:, j, :], ALU.mult, ALU.add)
                        k.copy(CbfS[dr][st_ % 8][:], Cst[dr][:], eng="act")
                bX = [[k.ps(), k.ps()], [k.ps(), k.ps()]]
                for dr in range(2):
                    for pos in range(4):
                        j = pos if dr == 0 else 3 - pos
                        st_ = 4 * g + pos
                        k.mm(bX[dr][j // 2][:, (j % 2) * 256:(j % 2) * 256 + 129], qT[:, tslf(dr, j)], CbfS[dr][(st_ - 1) % 8][:])
                for dr in range(2):
                    for j in range(4):
                        n = chunk(dr, j)
                        k.stt(totT[gb][dr][:, j, :], bX[dr][j // 2][:, (j % 2) * 256:(j % 2) * 256 + 129], EB[:, dr, n:n + 1],
                              intraT[gb][dr][:, j, :], ALU.mult, ALU.add)
                for dr in range(2):
                    k.act(dabT[gb][dr][:], totT[gb][dr][:, :, 128], AF.Abs)
                    k.ts(dabT[gb][dr][:], dabT[gb][dr][:], 1.0, ALU.max)
                    k.recip(dabT[gb][dr][:], dabT[gb][dr][:])
                    k.tt(hhT[gb][dr][:], totT[gb][dr][:, :, 0:128],
                         V(dabT[gb][dr].h[:].unsqueeze(2).broadcast_to([128, 4, 128]), dabT[gb][dr].r), ALU.mult)
                    rows = slice(lo[dr] * 128, (lo[dr] + 4) * 128)
                    k.dma(hD[dr].sub(rows).re("(j p) e -> p j e", p=128)[:], hhT[gb][dr][:], q="gq")

            NG = NCH // 4
            phase1(0)
            for g in range(NG):
                if g + 1 < NG:
                    phase1(g + 1)
                phase2(g)
            for g in range(NCH // GC):
                b = g % 2
                rows = slice(g * GC * 128, (g + 1) * GC * 128)
                k.dma(hfl[b][:], hD[0].sub(rows).re("(g p) e -> p g e", p=128)[:])
                k.dma(hbl[b][:], hD[1].sub(rows).re("(g p) e -> p g e", p=128)[:])
                k.dma(obT[b][:], osrc.sub(rows).re("(g p) e -> p g e", p=128)[:])
                k.tt(hfl[b][:], hfl[b][:], hbl[b][:], ALU.add)
                k.tt(hfl[b][:], hfl[b][:], obT[b][:], ALU.mult)
                k.rsum(s1T[b][:], hfl[b][:])
                k.ts(s1T[b][:], s1T[b][:], 1.0 / 128, ALU.mult)
                k.tt(hfl[b][:], hfl[b][:], V(s1T[b].h[:].unsqueeze(2).broadcast_to([128, GC, 128]), s1T[b].r), ALU.subtract)
                k.act(sqtT[b][:], hfl[b][:], AF.Square)
                k.rsum(s2T[b][:], sqtT[b][:])
                k.act(s2T[b][:], s2T[b][:], AF.Sqrt, bias=c["epsb"][:, 0:1], scale=1.0 / 128)
                k.recip(s2T[b][:], s2T[b][:])
                k.tt(hfl[b][:], hfl[b][:], V(s2T[b].h[:].unsqueeze(2).broadcast_to([128, GC, 128]), s2T[b].r), ALU.mult)
                k.tt(ybT[b][:], hfl[b][:], V(gain.h[:, h * 128:(h + 1) * 128].unsqueeze(1).broadcast_to([128, GC, 128]), gain.r),
                     ALU.mult)
                pT = k.ps()
                for j in range(GC):
                    k.mm(pT[:, j * 128:(j + 1) * 128], ybT[b][:, j, :], c["identb"][:])
                grp = g // 4
                yc = ycs[grp % 2]
                k.copy(yc[:, (g % 4) * 512:(g % 4 + 1) * 512], pT[:], eng="act")
                if g % 4 == 3:
                    k.dma(c["brT"][1024 + h * 128:1024 + (h + 1) * 128, sb + grp * 2048:sb + (grp + 1) * 2048],
                          yc[:], q="gq")


def stage3_out(k, layer, T, c):
    TS = 256
    with Stage(k) as S:
        wbr = S.sb([128, 16, DM], BF16)
        wo = S.sb([128, 8, DM], BF16)
        gpost = S.sb([128, 8], F32)
        k.dma(wbr[:], c["w_br"].sub(layer).re("(a p) d -> p a d", p=128)[:], q="gq")
        k.dma(wo[:], c["w_out"].sub(layer).re("(a p) d -> p a d", p=128)[:], q="gq")
        k.dma(gpost[:], c["n_post"].sub(layer).re("(kc p) -> p kc", p=128)[:], slow=True)
        brs = [S.sb([128, 16, TS], BF16) for _ in range(2)]
        gps = [S.sb([128, 16, TS], BF16) for _ in range(2)]
        mgs = [S.sb([128, 32, TS], BF16) for _ in range(2)]
        xs = [S.sb([128, 8, TS], F32) for _ in range(2)]
        gated = S.sb([128, 16, TS], BF16)
        mergeds = [S.sb([128, 8, TS], F32) for _ in range(2)]
        mbfs = [S.sb([128, 8, TS], BF16) for _ in range(2)]
        ot = S.sb([128, 8, TS], F32)
        sq = S.sb([128, 8, TS], BF16)
        tmps = [S.sb([128, 2, TS], F32) for _ in range(4)]
        rstd = S.sb([128, TS], F32)
        tic = [0]

        def partA(t):
            sl = slice(t * TS, (t + 1) * TS)
            br = brs[t % 2]; gp = gps[t % 2]; mg = mgs[t % 2]; x = xs[t % 2]
            merged = mergeds[t % 2]
            k.dma(br[:], fm(c["brT"])[:, :, sl])
            k.dma(gp[:], fm(c["gpT"])[:, :, sl])
            k.dma(mg[:], fm(c["mgT"])[:, :, sl])
            k.dma(x[:], fm(c["xT"])[:, :, sl])
            k.tt(gated[:], br[:], gp[:], ALU.mult)
            for g in range(4):
                for dbp in range(4):
                    pb = k.ps()
                    for hf in range(2):
                        db = 2 * dbp + hf
                        for cc in range(4):
                            k.mm(pb[:, hf * TS:(hf + 1) * TS], wbr[:, g * 4 + cc, db * 128:(db + 1) * 128], gated[:, g * 4 + cc, :],
                                 start=(cc == 0), stop=(cc == 3))
                    pv = pb.re("p (a n) -> p a n", a=2)
                    db0 = 2 * dbp
                    if g == 0:
                        k.tt(merged[:, db0:db0 + 2, :], pv[:], mg[:, g * 8 + db0:g * 8 + db0 + 2, :], ALU.mult)
                    else:
                        tmp = tmps[tic[0] % 4]; tic[0] += 1
                        k.tt(tmp[:], pv[:], mg[:, g * 8 + db0:g * 8 + db0 + 2, :], ALU.mult)
                        k.tt(merged[:, db0:db0 + 2, :], merged[:, db0:db0 + 2, :], tmp[:], ALU.add)
            k.copy(mbfs[t % 2][:], merged[:], eng="act")

        def partB(t):
            sl = slice(t * TS, (t + 1) * TS)
            x = xs[t % 2]; mbf = mbfs[t % 2]
            for d2p in range(4):
                pb = k.ps()
                for hf in range(2):
                    d2 = 2 * d2p + hf
                    for kc in range(8):
                        k.mm(pb[:, hf * TS:(hf + 1) * TS], wo[:, kc, d2 * 128:(d2 + 1) * 128], mbf[:, kc, :],
                             start=(kc == 0), stop=(kc == 7))
                k.copy(ot[:, 2 * d2p:2 * d2p + 2, :], pb.re("p (a n) -> p a n", a=2)[:], eng="act")
            k.act(sq[:], ot[:], AF.Square)
            pb = k.ps()
            for kc in range(8):
                k.mm(pb[:, 0:TS], c["onesb"][:], sq[:, kc, :], start=(kc == 0), stop=(kc == 7))
            k.act(rstd[:], pb[:, 0:TS], AF.Sqrt, bias=c["epsb"][:, 0:1], scale=1.0 / DM)
            k.recip(rstd[:], rstd[:])
            for kc in range(8):
                k.stt(ot[:, kc, :], ot[:, kc, :], gpost[:, kc:kc + 1], rstd[:], ALU.mult, ALU.mult)
            k.tt(x[:], x[:], ot[:], ALU.add)
            k.dma(fm(c["xT"])[:, :, sl], x[:], q="gq")

        n_t = T // TS
        partA(0)
        for t in range(n_t):
            if t + 1 < n_t:
                partA(t + 1)
            partB(t)


_CACHE = {}


def run_cores(seqs_per_core, mems_per_core, weights, seq_lens, depth, debug=False, run_layers=None):
    key = (tuple(seq_lens), depth, debug, run_layers)
    if key not in _CACHE:
        _CACHE[key] = build(list(seq_lens), depth, debug, run_layers)
    nc, cst = _CACHE[key]
    T = sum(seq_lens)
    idx = bias_idx()
    rb = np.asarray(weights["rel_bias"], np.float32)
    biasT = np.stack([rb[idx[g // 4], g] for g in range(12)]).astype(np.float32)
    shared = {
        "w_in": np.ascontiguousarray(weights["w_in"][:depth], np.float32),
        "w_mem_kv": np.ascontiguousarray(weights["w_mem_kv"][:depth], np.float32),
        "w_branch": np.ascontiguousarray(weights["w_branch"][:depth], np.float32).reshape(depth, 2048, DM),
        "w_out": np.ascontiguousarray(weights["w_out"][:depth], np.float32),
        "norm_pre": np.ascontiguousarray(weights["norm_pre"][:depth], np.float32),
        "norm_post": np.ascontiguousarray(weights["norm_post"][:depth], np.float32),
        "mem_norm": np.ascontiguousarray(weights["mem_norm"][:depth], np.float32),
        "conv_qk": np.ascontiguousarray(weights["conv_qk"][:depth], np.float32),
        "gate_bias": np.ascontiguousarray(weights["mlstm_gate_bias"][:depth], np.float32).reshape(depth, 16),
        "head_gain": np.ascontiguousarray(weights["mlstm_head_gain"][:depth], np.float32),
        "biasT": biasT,
    }
    for kk, v in cst.items():
        shared["c_" + kk] = v
    in_maps = []
    for ci in range(8):
        xin = np.zeros((T, DM), np.float32)
        memin = np.zeros((len(seq_lens) * 256, DM), np.float32)
        off = 0
        for si, L in enumerate(seq_lens):
            if seqs_per_core[ci][si] is not None:
                xin[off:off + L] = seqs_per_core[ci][si]
                memin[si * 256:(si + 1) * 256] = mems_per_core[ci][si]
            off += L
        m = dict(shared)
        m["xin"] = xin
        m["memin"] = memin
        in_maps.append(m)
    res = run_bass_kernel_spmd(nc, in_maps, core_ids=list(range(8)))
    return res.results


def kernel(x_prompt, x_sample, mem_prompt, mem_sample, rel_bias, norm_pre, w_in, conv_qk,
           mlstm_gate_bias, mlstm_head_gain, mem_norm, w_mem_kv, w_branch, w_out, norm_post):
    weights = dict(rel_bias=rel_bias, norm_pre=norm_pre, w_in=w_in, conv_qk=conv_qk,
                   mlstm_gate_bias=mlstm_gate_bias, mlstm_head_gain=mlstm_head_gain, mem_norm=mem_norm,
                   w_mem_kv=w_mem_kv, w_branch=w_branch, w_out=w_out, norm_post=norm_post)
    x_prompt = np.asarray(x_prompt); x_sample = np.asarray(x_sample)
    LP, LS = x_prompt.shape[1], x_sample.shape[1]
    seqs = [[None, None] for _ in range(8)]
    mems = [[None, None] for _ in range(8)]
    for b in range(x_prompt.shape[0]):
        seqs[b][0] = x_prompt[b]; mems[b][0] = np.asarray(mem_prompt[b])
    for s in range(x_sample.shape[0]):
        seqs[4 + s][1] = x_sample[s]; mems[4 + s][1] = np.asarray(mem_sample[s])
    res = run_cores(seqs, mems, weights, (LP, LS), int(np.asarray(w_in).shape[0]))
    yp = np.stack([res[b]["yout"][0:LP] for b in range(x_prompt.shape[0])]).astype(np.float32)
    ys = np.stack([res[4 + s]["yout"][LP:LP + LS] for s in range(x_sample.shape[0])]).astype(np.float32)
    return (yp, ys)
```
